# Optimizing a Trainium2 kernel written in Bass

```python
import math
import jax, jax.numpy as jnp
from jax import lax
import numpy as np

D_MODEL = 4096
BATCH = 2
SEQ = 4096
DEPTH = 1

N_META = 16
ATTN_WIDTH = D_MODEL // 2
LRU_WIDTH = D_MODEL - ATTN_WIDTH
DIFF_QK_DIM = 128
DIFF_V_DIM = 2 * DIFF_QK_DIM
DIFF_HEADS = ATTN_WIDTH // DIFF_V_DIM
LRU_HEADS = 16
LRU_BLOCK = LRU_WIDTH // LRU_HEADS
CONV_WIDTH = 4
LRU_C = 8.0
D_FF = ((-(-8 * D_MODEL // 3)) + 255) // 256 * 256
IN_WIDTH = 3 * ATTN_WIDTH + 2 * LRU_WIDTH
ROPE_THETA = 10000.0
BLOCK_Q = 128
RMS_EPS = 1e-6

kernel_name = 'hymba_diffattn_rglru_sandwich'


def rms_norm(x, g):
    xf = x.astype(jnp.float32)
    y = xf * lax.rsqrt(jnp.mean(xf * xf, axis=-1, keepdims=True) + RMS_EPS)
    return (y * g.astype(jnp.float32)).astype(x.dtype)


def rotary_tables(T):
    inv_freq = 1.0 / (ROPE_THETA ** (jnp.arange(0, DIFF_QK_DIM, 2, dtype=jnp.float32) / DIFF_QK_DIM))
    ang = jnp.arange(T, dtype=jnp.float32)[:, None] * inv_freq[None, :]
    return jnp.cos(ang), jnp.sin(ang)


def apply_rope(x, cos, sin):
    half = x.shape[-1] // 2
    xf = x.astype(jnp.float32)
    x1, x2 = xf[..., :half], xf[..., half:]
    c = cos[None, :, None, None, :]
    s = sin[None, :, None, None, :]
    return jnp.concatenate([x1 * c - x2 * s, x2 * c + x1 * s], axis=-1).astype(x.dtype)


def diff_attention(q, k, v, lam, subln_g, lambda_init):
    B, T = q.shape[0], q.shape[1]
    scale = DIFF_QK_DIM ** -0.5
    outs = []
    for start in range(0, T, BLOCK_Q):
        end = start + BLOCK_Q
        qb = q[:, start:end]
        kb = k[:, :end]
        vb = v[:, :end]
        s = jnp.einsum('bqhcd,bkhcd->bhcqk', qb, kb).astype(jnp.float32) * scale
        qpos = jnp.arange(start, end)
        kpos = jnp.arange(end)
        mask = kpos[None, :] <= qpos[:, None]
        s = jnp.where(mask, s, -jnp.inf)
        p = jax.nn.softmax(s, axis=-1)
        w = p[:, :, 0] - lam * p[:, :, 1]
        outs.append(jnp.einsum('bhqk,bkhd->bqhd', w.astype(v.dtype), vb))
    o = jnp.concatenate(outs, axis=1)
    o = rms_norm(o, subln_g) * (1.0 - lambda_init)
    return o.reshape(B, T, DIFF_HEADS * DIFF_V_DIM)


def rg_lru_branch(xr, gate, conv_w, conv_b, w_r, b_r, w_i, b_i, lru_lambda):
    B, T, C = xr.shape
    xc = lax.conv_general_dilated(
        xr, conv_w[:, None, :].astype(xr.dtype), window_strides=(1,),
        padding=[(CONV_WIDTH - 1, 0)], dimension_numbers=('NWC', 'WIO', 'NWC'),
        feature_group_count=C) + conv_b
    xh = xc.reshape(B, T, LRU_HEADS, LRU_BLOCK)
    r = jax.nn.sigmoid(jnp.einsum('bthi,hij->bthj', xh, w_r).reshape(B, T, C) + b_r)
    i = jax.nn.sigmoid(jnp.einsum('bthi,hij->bthj', xh, w_i).reshape(B, T, C) + b_i)
    log_a = -LRU_C * r.astype(jnp.float32) * jax.nn.softplus(-lru_lambda.astype(jnp.float32))
    a = jnp.exp(log_a)
    mult = jnp.sqrt(-jnp.expm1(2.0 * log_a))
    bterm = mult * (i * xc).astype(jnp.float32)

    def combine(left, right):
        a_l, b_l = left
        a_r, b_r2 = right
        return a_l * a_r, a_r * b_l + b_r2

    _, h = lax.associative_scan(combine, (a, bterm), axis=1)
    return h.astype(xr.dtype) * jax.nn.gelu(gate)


def setup_inputs(seed: int = 0) -> dict:
    key = jax.random.key(seed)
    ks = jax.random.split(key, 24)
    f32 = jnp.float32
    nrm = lambda k, shp, sc: jax.random.normal(k, shp, f32) * sc
    u = jax.random.uniform(ks[14], (DEPTH, LRU_WIDTH), f32, 0.9, 0.999)
    a0 = u ** (1.0 / LRU_C)
    lru_lambda = jnp.log(a0) - jnp.log1p(-a0)
    return {
        'x': nrm(ks[0], (BATCH, SEQ, D_MODEL), 1.0),
        'meta_tokens': nrm(ks[1], (N_META, D_MODEL), 1.0),
        'mix_pre_g': 1.0 + nrm(ks[2], (DEPTH, D_MODEL), 0.02),
        'w_in': nrm(ks[3], (DEPTH, D_MODEL, IN_WIDTH), D_MODEL ** -0.5),
        'lambda_q1': nrm(ks[4], (DEPTH, DIFF_QK_DIM), 0.1),
        'lambda_k1': nrm(ks[5], (DEPTH, DIFF_QK_DIM), 0.1),
        'lambda_q2': nrm(ks[6], (DEPTH, DIFF_QK_DIM), 0.1),
        'lambda_k2': nrm(ks[7], (DEPTH, DIFF_QK_DIM), 0.1),
        'subln_g': 1.0 + nrm(ks[8], (DEPTH, DIFF_V_DIM), 0.02),
        'conv_w': nrm(ks[9], (DEPTH, CONV_WIDTH, LRU_WIDTH), CONV_WIDTH ** -0.5),
        'conv_b': nrm(ks[10], (DEPTH, LRU_WIDTH), 0.02),
        'w_r': nrm(ks[11], (DEPTH, LRU_HEADS, LRU_BLOCK, LRU_BLOCK), LRU_BLOCK ** -0.5),
        'b_r': nrm(ks[12], (DEPTH, LRU_WIDTH), 0.02),
        'w_i': nrm(ks[13], (DEPTH, LRU_HEADS, LRU_BLOCK, LRU_BLOCK), LRU_BLOCK ** -0.5),
        'b_i': nrm(ks[15], (DEPTH, LRU_WIDTH), 0.02),
        'lru_lambda': lru_lambda,
        'w_out': nrm(ks[16], (DEPTH, ATTN_WIDTH + LRU_WIDTH, D_MODEL), (ATTN_WIDTH + LRU_WIDTH) ** -0.5),
        'mix_post_g': 1.0 + nrm(ks[17], (DEPTH, D_MODEL), 0.02),
        'ffn_pre_g': 1.0 + nrm(ks[18], (DEPTH, D_MODEL), 0.02),
        'w_gate': nrm(ks[19], (DEPTH, D_MODEL, D_FF), D_MODEL ** -0.5),
        'w_up': nrm(ks[20], (DEPTH, D_MODEL, D_FF), D_MODEL ** -0.5),
        'w_down': nrm(ks[21], (DEPTH, D_FF, D_MODEL), D_FF ** -0.5),
        'ffn_post_g': 1.0 + nrm(ks[22], (DEPTH, D_MODEL), 0.02),
    }


def reference(x, meta_tokens, mix_pre_g, w_in, lambda_q1, lambda_k1, lambda_q2, lambda_k2,
              subln_g, conv_w, conv_b, w_r, b_r, w_i, b_i, lru_lambda, w_out, mix_post_g,
              ffn_pre_g, w_gate, w_up, w_down, ffn_post_g):
    B, S, D = x.shape
    T = N_META + S
    T_pad = -(-T // BLOCK_Q) * BLOCK_Q
    meta = jnp.broadcast_to(meta_tokens[None].astype(x.dtype), (B, N_META, D))
    h = jnp.concatenate([meta, x, jnp.zeros((B, T_pad - T, D), x.dtype)], axis=1)
    cos, sin = rotary_tables(T_pad)
    splits = [ATTN_WIDTH, 2 * ATTN_WIDTH, 3 * ATTN_WIDTH, 3 * ATTN_WIDTH + LRU_WIDTH]
    for l in range(DEPTH):
        lambda_init = 0.8 - 0.6 * math.exp(-0.3 * l)
        u = rms_norm(h, mix_pre_g[l])
        proj = u @ w_in[l]
        q, k, v, xr, gate = jnp.split(proj, splits, axis=-1)
        q = apply_rope(q.reshape(B, T_pad, DIFF_HEADS, 2, DIFF_QK_DIM), cos, sin)
        k = apply_rope(k.reshape(B, T_pad, DIFF_HEADS, 2, DIFF_QK_DIM), cos, sin)
        v = v.reshape(B, T_pad, DIFF_HEADS, DIFF_V_DIM)
        lam = (jnp.exp(jnp.sum(lambda_q1[l].astype(jnp.float32) * lambda_k1[l].astype(jnp.float32)))
               - jnp.exp(jnp.sum(lambda_q2[l].astype(jnp.float32) * lambda_k2[l].astype(jnp.float32)))
               + lambda_init)
        attn = diff_attention(q, k, v, lam, subln_g[l], lambda_init)
        rec = rg_lru_branch(xr, gate, conv_w[l], conv_b[l], w_r[l], b_r[l],
                            w_i[l], b_i[l], lru_lambda[l])
        mixed = jnp.concatenate([attn, rec], axis=-1) @ w_out[l]
        h = h + rms_norm(mixed, mix_post_g[l])
        u = rms_norm(h, ffn_pre_g[l])
        f = (jax.nn.silu(u @ w_gate[l]) * (u @ w_up[l])) @ w_down[l]
        h = h + rms_norm(f, ffn_post_g[l])
    return h[:, N_META:N_META + S]
```

```python
import math
from contextlib import ExitStack

import numpy as np
import ml_dtypes

import concourse.bass as bass
import concourse.mybir as mybir
from concourse.bass_utils import run_bass_kernel_spmd

F32 = mybir.dt.float32
BF16 = mybir.dt.bfloat16
AF = mybir.ActivationFunctionType
ALU = mybir.AluOpType
AX = mybir.AxisListType

NCORES = 8
D = 4096
KT = D // 128
FF = 11008
FT = FF // 128
S_LEN = 4096
NBATCH = 2
NMETA = 16
TPOS = NMETA + S_LEN
TOK_PER_CORE = 1024
RMS_EPS = 1e-6
LAMBDA_INIT = 0.8 - 0.6 * math.exp(-0.3 * 0)
ATT_SCALE = 128 ** -0.5


def _is_tok(d):
    return isinstance(d, tuple) and len(d) == 2 and isinstance(d[1], int) and isinstance(d[0], tuple)


def _flat(deps, out):
    for d in deps:
        if d is None:
            continue
        if _is_tok(d):
            out.append(d)
        else:
            _flat(d, out)
    return out


class Sched:
    ENG = ("pe", "act", "dve", "pool", "sp")

    def __init__(self, nc, stack, ndma=8):
        self.nc = nc
        self.q = {e: [] for e in self.ENG}
        self.sems = {}
        for e in ("pe", "act", "dve", "pool"):
            self.sems[("e", e)] = stack.enter_context(nc.semaphore(f"es_{e}"))
        self.ecnt = {e: 0 for e in ("pe", "act", "dve", "pool")}
        self.ndma = ndma
        self.dcnt = {}
        self.dnext = {}
        for e in ("sp", "pool", "act"):
            for i in range(ndma):
                self.sems[("d", e, i)] = stack.enter_context(nc.semaphore(f"ds_{e}{i}"))
                self.dcnt[(e, i)] = 0
            self.dnext[e] = 0
        self.waited = {e: {} for e in self.ENG}
        self.need_pid = set()
        self.pid = {}
        self.sems[("cc",)] = stack.enter_context(nc.semaphore("cc_sem"))
        self.cc_cnt = 0

    def collective(self, fn, deps=()):
        waits = self._waits("pool", deps)
        self.cc_cnt += 1
        self.q["pool"].append((waits, fn, ("cc",), 1))
        return (("cc",), self.cc_cnt)

    def all_done_tokens(self):
        toks = [(("e", e), self.ecnt[e]) for e in self.ecnt if self.ecnt[e] > 0]
        toks += [(("d", e, i), self.dcnt[(e, i)]) for (e, i) in self.dcnt if self.dcnt[(e, i)] > 0]
        if self.cc_cnt:
            toks.append((("cc",), self.cc_cnt))
        return toks

    def barrier(self, toks):
        for e in self.ENG:
            waits = self._waits(e, toks)
            if waits:
                self.q[e].append((waits, None, None, 0))

    def _waits(self, eng, deps):
        out = []
        for key, val in _flat(deps, []):
            if key == ("e", eng) and eng == "pe":
                continue
            if self.waited[eng].get(key, 0) >= val:
                continue
            self.waited[eng][key] = val
            out.append((key, val))
        return out

    def op(self, eng, fn, deps=(), signal=None):
        if signal is None:
            signal = eng != "pe"
        waits = self._waits(eng, deps)
        tok = None
        key = None
        if signal:
            self.ecnt[eng] += 1
            key = ("e", eng)
            tok = (key, self.ecnt[eng])
        self.q[eng].append((waits, fn, key, 1))
        return tok

    def dma(self, eng, out, in_, deps=()):
        i = self.dnext[eng]
        self.dnext[eng] = (i + 1) % self.ndma
        key = ("d", eng, i)
        prev = self.dcnt[(eng, i)]
        deps = [deps]
        if prev > 0:
            deps.append((key, prev))
        waits = self._waits(eng, deps)
        self.dcnt[(eng, i)] = prev + 16
        self.q[eng].append((waits, (lambda e, o=out, s=in_: e.dma_start(out=o, in_=s)), key, 16))
        return (key, prev + 16)

    def finalize(self):
        for e in ("sp", "pool", "act"):
            deps = [(("d", e, i), self.dcnt[(e, i)]) for i in range(self.ndma) if self.dcnt[(e, i)] > 0]
            waits = self._waits(e, deps)
            if waits:
                self.q[e].append((waits, None, None, 0))

    def run(self, block):
        m = {"pe": block.tensor, "act": block.scalar, "dve": block.vector, "pool": block.gpsimd, "sp": block.sync}
        for eng in self.ENG:
            ops = self.q[eng]

            def body(e, ops=ops, eng=eng):
                if eng in self.need_pid:
                    self.pid[eng] = e.partition_id()
                for waits, fn, key, inc in ops:
                    for (k, v) in waits:
                        e.wait_ge(self.sems[k], v)
                    if fn is None:
                        continue
                    ins = fn(e)
                    if key is not None:
                        ins.then_inc(self.sems[key], inc)

            m[eng](body)


class Arena:
    NBYTES = 207 * 1024

    def __init__(self, nc, stack):
        self.t = stack.enter_context(nc.sbuf_tensor("arena", [128, self.NBYTES // 2], BF16))
        self.off = 0

    def reset(self):
        self.off = 0

    def alloc(self, shape, dt):
        esz = 4 if dt == F32 else 2
        n = 1
        for d_ in shape[1:]:
            n *= d_
        nbytes = (n * esz + 31) // 32 * 32
        assert self.off + nbytes <= self.NBYTES, f"arena overflow {self.off + nbytes}"
        ap = self.t[:, self.off // 2:(self.off + n * esz) // 2]
        self.off += nbytes
        if dt != BF16:
            ap = ap.bitcast(dt)
        if len(shape) == 3:
            ap = ap.rearrange("p (a b) -> p a b", a=shape[1])
        elif len(shape) == 4:
            ap = ap.rearrange("p (a b c) -> p a b c", a=shape[1], b=shape[2])
        if shape[0] != 128:
            ap = ap[0:shape[0]]
        return ap


class Users:
    def __init__(self, n=1):
        self.u = [[] for _ in range(n)]

    def take(self, i=0):
        r = self.u[i]
        self.u[i] = []
        return r

    def add(self, i, *toks):
        self.u[i].extend(t for t in toks if t is not None)


class Banks:
    def __init__(self, nc, stack):
        self.t = [stack.enter_context(nc.psum_tensor(f"ps{i}", [128, 512], F32)) for i in range(8)]
        self.last = [None] * 8

    def dep(self, i):
        return self.last[i]

    def set(self, i, tok):
        if tok is not None:
            self.last[i] = tok


def emit_rstd(S, ss_ap, n, out_ap, tmp_ap, eps_ap, deps):
    t = S.op("dve", lambda e: e.tensor_reduce(out=tmp_ap, in_=ss_ap, op=ALU.add, axis=AX.X), deps=deps)
    t = S.op("act", lambda e: e.activation(out=tmp_ap, in_=tmp_ap, func=AF.Ln, bias=eps_ap, scale=1.0 / n), deps=[t])
    t = S.op("act", lambda e: e.activation(out=out_ap, in_=tmp_ap, func=AF.Exp, scale=-0.5), deps=[t])
    return t


def emit_phase_b(nc, S, stack, PS, io, AR, cat_ready=(), fused=False):
    catT = io["catT"]
    xrows = io["xrows"]
    w_out, w_gate, w_up, w_down = io["w_out"], io["w_gate"], io["w_up"], io["w_down"]
    g_post, g2T_d, g_fpost = io["g_post"], io["g2T"], io["g_fpost"]
    out = io["out"]
    mixed_s, f_s, h1_s = io["mixed_s"], io["f_s"], io["h1_s"]
    wout_rt = io["wout_rt"]

    TT, TB, NTT = 512, 4, TOK_PER_CORE // 512
    PW = 512
    NPC = D // PW

    def sb(name, shape, dt):
        return AR.alloc(shape, dt)

    hffT = sb("hffT", [128, FT, TT], BF16)
    u2T = sb("u2T", [128, KT, TT], BF16)
    NW = 4
    wring = [sb(f"wr{i}", [128, 4096], BF16) for i in range(NW)]
    wr_u = Users(NW)
    wr_i = [0]
    rbH = sb("rbH", [128, D], F32)
    rbM = [sb(f"rbM{i}", [128, PW], F32) for i in range(2)]
    rbX = [sb(f"rbX{i}", [128, PW], F32) for i in range(2)]
    rbG = [sb(f"rbG{i}", [128, PW], F32) for i in range(2)]
    u2p = [sb(f"u2p{i}", [128, PW], BF16) for i in range(2)]
    mc = [sb(f"mc{i}", [128, 512], F32) for i in range(2)]
    sg = [sb(f"sg{i}", [128, 512], F32) for i in range(2)]
    fT_sb = [sb(f"fT{i}", [128, 4, TT], F32) for i in range(2)]
    junk = sb("junkb", [128, 512], BF16)
    ss1 = sb("ss1", [128, TB, 8], F32)
    ss2 = sb("ss2", [128, TB, NPC], F32)
    ss3 = sb("ss3", [128, TB, 8], F32)
    rs = sb("rs", [128, 16], F32)
    tmpv = sb("tmpv", [128, 16], F32)
    epsb = sb("epsb", [128, 1], F32)
    g2T = sb("g2T_sb", [128, KT], F32)
    identb = sb("identb_sb", [128, 128], BF16)
    identf = sb("identf_sb", [128, 128], F32)
    rbM_u, rbX_u, rbG_u, rbO_u, u2p_u = Users(2), Users(2), Users(2), Users(2), Users(2)
    mc_u, sg_u, fT_u = Users(2), Users(2), Users(2)
    rbH_u = Users(1)
    junk_t = [None]

    t_eps = S.op("dve", lambda e: e.memset(epsb[:, :], RMS_EPS))
    t_g2 = S.dma("sp", g2T[:, :], g2T_d[:, :])
    t_ib = S.dma("sp", identb[:, :], io["identb"][:, :])
    t_if = S.dma("sp", identf[:, :], io["identf"][:, :])

    def wload(view_fn, src_ap):
        i = wr_i[0] % NW
        wr_i[0] += 1
        tok = S.dma("pool", view_fn(wring[i]), src_ap, deps=wr_u.take(i))
        return i, tok

    wv_out = w_out.rearrange("(kt p) c -> p kt c", p=128)
    wv_gate = w_gate.rearrange("(kt p) c -> p kt c", p=128)
    wv_up = w_up.rearrange("(kt p) c -> p kt c", p=128)
    wv_down = w_down.rearrange("(j p) c -> p j c", p=128)
    catv = catT.rearrange("(kt p) t -> p kt t", p=128) if catT is not None else None

    u2T_readers = []
    hff_readers = []
    h1_written = {}

    for tt in range(NTT):
        r0 = tt * TT
        cat_toks = [None] * KT
        if not fused:
            for q4 in range(4):
                tq4 = S.dma("sp", u2T[:, q4 * 8:(q4 + 1) * 8, :], catv[:, q4 * 8:(q4 + 1) * 8, r0:r0 + TT],
                            deps=[u2T_readers, cat_ready])
                for k_ in range(8):
                    cat_toks[q4 * 8 + k_] = tq4
        else:
            for half, agt in enumerate((io["ag_attn"], io["ag_rec"])):
                waits = S._waits("sp", [u2T_readers, cat_ready])
                i_ = S.dnext["sp"]
                S.dnext["sp"] = (i_ + 1) % S.ndma
                key = ("d", "sp", i_)
                prev = S.dcnt[("sp", i_)]
                waits += S._waits("sp", [(key, prev)] if prev > 0 else [])
                S.dcnt[("sp", i_)] = prev + 16
                S.need_pid.add("sp")
                S.q["sp"].append((waits, (lambda e, agt=agt, half=half, r0=r0: e.dma_start(
                    out=u2T[:, half * 16:(half + 1) * 16, :].rearrange("p (k o) w -> p k o w", o=1),
                    in_=agt.rearrange("(k j p) w -> p k j w", k=16, j=NCORES, p=128)[:, :, bass.ds(S.pid["sp"], 1), r0:r0 + TT])), key, 16))
                for k_ in range(16):
                    cat_toks[half * 16 + k_] = (key, prev + 16)
        u2T_readers = []
        mixed_w = {}
        for c in range(8):
            bset = [0, 1, 2, 3] if c % 2 == 0 else [4, 5, 6, 7]
            last = [None] * TB
            for pc in range(4):
                rt0 = pc * 8
                wi, wt = wload(lambda w: w[:, :].rearrange("p (k c) -> p k c", k=8),
                               wv_out[:, rt0:rt0 + 8, c * 512:(c + 1) * 512])
                wv = wring[wi][:, :].rearrange("p (k c) -> p k c", k=8)
                for tb in range(TB):
                    for k8 in range(8):
                        kt = pc * 8 + k8
                        first = (pc == 0 and k8 == 0)
                        lastmm = (pc == 3 and k8 == 7)
                        deps = [wt, cat_toks[kt]]
                        if first:
                            deps.append(PS.dep(bset[tb]))
                        sig = lastmm or (tb == TB - 1 and k8 == 7)
                        tk = S.op("pe", (lambda e, b=bset[tb], kt=kt, tb=tb, k8=k8, wv=wv, first=first, lastmm=lastmm:
                                         e.matmul(PS.t[b][:, :], lhsT=u2T[:, kt, tb * 128:(tb + 1) * 128], rhs=wv[:, k8, :],
                                                  start=first, stop=lastmm)),
                                  deps=deps, signal=sig)
                        if lastmm:
                            last[tb] = tk
                            PS.set(bset[tb], tk)
                wr_u.add(wi, tk)
            u2T_readers.append(last[TB - 1])
            for tb in range(TB):
                b = bset[tb]
                mi = (c * TB + tb) % 2
                t1 = S.op("act", (lambda e, b=b, tb=tb, c=c: e.activation(out=junk[:, :], in_=PS.t[b][:, :], func=AF.Square,
                                                                    accum_out=ss1[:, tb, c:c + 1])),
                          deps=[PS.dep(b)])
                junk_t[0] = t1
                t2 = S.op("act", (lambda e, b=b, mi=mi: e.activation(out=mc[mi][:, :], in_=PS.t[b][:, :], func=AF.Copy)),
                          deps=[t1, mc_u.take(mi)])
                PS.set(b, t2)
                tw = S.dma("sp", mixed_s[r0 + tb * 128:r0 + (tb + 1) * 128, c * 512:(c + 1) * 512], mc[mi][:, :], deps=[t2])
                mc_u.add(mi, tw)
                mixed_w[(tb, c)] = tw

        u2_w = []
        for tb in range(TB):
            rows = slice(r0 + tb * 128, r0 + (tb + 1) * 128)
            t_r1 = emit_rstd(S, ss1[:, tb, :], D, rs[:, 0:1], tmpv[:, 0:1], epsb[:, 0:1], deps=[junk_t[0], t_eps])
            sq_toks = []
            hdeps = rbH_u.take(0)
            tlast = None
            for pc in range(NPC):
                cs = slice(pc * PW, (pc + 1) * PW)
                i2 = pc % 2
                lm = S.dma("sp", rbM[i2][:, :], mixed_s[rows, cs], deps=[mixed_w[(tb, pc)], rbM_u.take(i2)])
                lx = S.dma("sp", rbX[i2][:, :], xrows[rows, cs], deps=[rbX_u.take(i2)])
                lg = S.dma("sp", rbG[i2][:, :], g_post[0, cs].partition_broadcast(128), deps=[rbG_u.take(i2)])
                ta = S.op("dve", (lambda e, i2=i2: e.scalar_tensor_tensor(out=rbM[i2][:, :], in0=rbM[i2][:, :], scalar=rs[:, 0:1],
                                                                           in1=rbG[i2][:, :], op0=ALU.mult, op1=ALU.mult)),
                          deps=[lm, lg, t_r1])
                rbG_u.add(i2, ta)
                tb_ = S.op("dve", (lambda e, i2=i2, cs=cs: e.tensor_tensor(out=rbH[:, cs], in0=rbM[i2][:, :], in1=rbX[i2][:, :], op=ALU.add)),
                           deps=[ta, lx, hdeps])
                rbM_u.add(i2, tb_)
                rbX_u.add(i2, tb_)
                tq = S.op("act", (lambda e, cs=cs, tb=tb, pc=pc: e.activation(out=junk[:, :], in_=rbH[:, cs], func=AF.Square,
                                                                             accum_out=ss2[:, tb, pc:pc + 1])),
                          deps=[tb_])
                junk_t[0] = tq
                sq_toks.append(tq)
                tlast = tb_
            th = S.dma("sp", h1_s[rows, :], rbH[:, :], deps=[tlast])
            h1_written[(tt, tb)] = th
            t_r2 = emit_rstd(S, ss2[:, tb, :], D, rs[:, 1:2], tmpv[:, 1:2], epsb[:, 0:1], deps=sq_toks)
            tu = None
            for pc in range(NPC):
                cs = slice(pc * PW, (pc + 1) * PW)
                i2 = pc % 2
                tu = S.op("act", (lambda e, i2=i2, cs=cs: e.activation(out=u2p[i2][:, :], in_=rbH[:, cs], func=AF.Copy, scale=rs[:, 1:2])),
                          deps=[t_r2, u2p_u.take(i2)])
                b = 4 + (pc % 2)
                pv = PS.t[b][:, 0:256].bitcast(BF16).rearrange("p (a t) -> p a t", a=4)
                tp = None
                for j in range(4):
                    tp = S.op("pe", (lambda e, i2=i2, j=j, pv=pv: e.transpose(out=pv[:, j, :], in_=u2p[i2][:, j * 128:(j + 1) * 128],
                                                                            identity=identb[:, :])),
                              deps=[tu, t_ib, PS.dep(b) if j == 0 else None], signal=(j == 3))
                u2p_u.add(i2, tp)
                kt0 = pc * 4
                te = S.op("dve", (lambda e, pv=pv, kt0=kt0, tb=tb: e.tensor_tensor(
                    out=u2T[:, kt0:kt0 + 4, tb * 128:(tb + 1) * 128], in0=pv,
                    in1=g2T[:, kt0:kt0 + 4].unsqueeze(2).to_broadcast([128, 4, 128]), op=ALU.mult)),
                          deps=[tp, t_g2, u2T_readers if (pc == 0 and tb == 0) else None])
                PS.set(b, te)
                u2_w.append(te)
            rbH_u.add(0, th, tu)
        u2T_readers = []

        hdeps_once = hff_readers
        hff_readers = []
        hff_w = {}
        for j2 in range(FT // 2):
            bset = [0, 1, 2, 3] if j2 % 2 == 0 else [4, 5, 6, 7]
            last = [None] * 4
            for mi_, wv_src in ((0, wv_gate), (1, wv_up)):
                for half in range(2):
                    wi, wt = wload(lambda w: w[:, :].rearrange("p (k c) -> p k c", k=16),
                                   wv_src[:, half * 16:(half + 1) * 16, j2 * 256:(j2 + 1) * 256])
                    wv = wring[wi][:, :].rearrange("p (k c) -> p k c", k=16)
                    for f in range(2):
                        bi = mi_ * 2 + f
                        for k in range(16):
                            kt = half * 16 + k
                            first = (half == 0 and k == 0)
                            lastmm = (half == 1 and k == 15)
                            deps = [wt]
                            if first:
                                deps.append(PS.dep(bset[bi]))
                                if j2 == 0 and mi_ == 0 and f == 0:
                                    deps.append(u2_w)
                            sig = lastmm or (f == 1 and k == 15)
                            tk = S.op("pe", (lambda e, b=bset[bi], kt=kt, k=k, f=f, wv=wv, first=first, lastmm=lastmm:
                                             e.matmul(PS.t[b][:, :], lhsT=wv[:, k, f * 128:(f + 1) * 128], rhs=u2T[:, kt, :],
                                                      start=first, stop=lastmm)),
                                      deps=deps, signal=sig)
                            if lastmm:
                                last[bi] = tk
                                PS.set(bset[bi], tk)
                    wr_u.add(wi, tk)
            u2T_readers.append(last[3])
            for f in range(2):
                j = j2 * 2 + f
                si = j % 2
                t1 = S.op("act", (lambda e, b=bset[f], si=si: e.activation(out=sg[si][:, :], in_=PS.t[b][:, :], func=AF.Silu)),
                          deps=[last[f], sg_u.take(si)])
                PS.set(bset[f], t1)
                t2 = S.op("dve", (lambda e, b=bset[2 + f], si=si, j=j: e.tensor_tensor(out=hffT[:, j, :], in0=PS.t[b][:, :], in1=sg[si][:, :],
                                                                                     op=ALU.mult)),
                          deps=[t1, last[2 + f], hdeps_once if j == 0 else None])
                sg_u.add(si, t2)
                PS.set(bset[2 + f], t2)
                hff_w[j] = t2

        f_w = {}
        pend = None
        NJP = (FT + 7) // 8

        def emit_transposes(g, fi, t_ev):
            for tb in range(TB):
                b = 4 + tb
                tp = None
                for dt_ in range(4):
                    tp = S.op("pe", (lambda e, b=b, dt_=dt_, fi=fi, tb=tb: e.transpose(
                        out=PS.t[b][:, dt_ * 128:(dt_ + 1) * 128], in_=fT_sb[fi][:, dt_, tb * 128:(tb + 1) * 128], identity=identf[:, :])),
                              deps=[t_ev[dt_], t_if, PS.dep(b) if dt_ == 0 else None], signal=(dt_ == 3))
                fT_u.add(fi, tp)
                t1 = S.op("act", (lambda e, b=b, tb=tb, g=g: e.activation(out=junk[:, :], in_=PS.t[b][:, :], func=AF.Square,
                                                                         accum_out=ss3[:, tb, g:g + 1])),
                          deps=[tp])
                junk_t[0] = t1
                mi = (g * TB + tb) % 2
                t2 = S.op("act", (lambda e, b=b, mi=mi: e.activation(out=mc[mi][:, :], in_=PS.t[b][:, :], func=AF.Copy)),
                          deps=[t1, mc_u.take(mi)])
                PS.set(b, t2)
                tw = S.dma("sp", f_s[r0 + tb * 128:r0 + (tb + 1) * 128, g * 512:(g + 1) * 512], mc[mi][:, :], deps=[t2])
                mc_u.add(mi, tw)
                f_w[(tb, g)] = tw

        for g in range(8):
            bset = [0, 1, 2, 3]
            last = [None] * 4
            for jp in range(NJP):
                j0 = jp * 8
                nj = min(8, FT - j0)
                wi, wt = wload(lambda w, nj=nj: w[:, 0:nj * 512].rearrange("p (k c) -> p k c", k=nj),
                               wv_down[:, j0:j0 + nj, g * 512:(g + 1) * 512])
                wv = wring[wi][:, 0:nj * 512].rearrange("p (k c) -> p k c", k=nj)
                for jj in range(nj):
                    j = j0 + jj
                    for dt_ in range(4):
                        first = (j == 0)
                        lastmm = (j == FT - 1)
                        deps = [wt, hff_w[j]]
                        if first:
                            deps.append(PS.dep(bset[dt_]))
                        sig = lastmm or (jj == nj - 1 and dt_ == 3)
                        tk = S.op("pe", (lambda e, b=bset[dt_], jj=jj, j=j, dt_=dt_, wv=wv, first=first, lastmm=lastmm:
                                         e.matmul(PS.t[b][:, :], lhsT=wv[:, jj, dt_ * 128:(dt_ + 1) * 128], rhs=hffT[:, j, :],
                                                  start=first, stop=lastmm)),
                                  deps=deps, signal=sig)
                        if lastmm:
                            last[dt_] = tk
                            PS.set(bset[dt_], tk)
                wr_u.add(wi, tk)
                if jp == 1 and pend is not None:
                    emit_transposes(*pend)
                    pend = None
            hff_readers.append(last[3])
            fi = g % 2
            t_ev = []
            fdeps = fT_u.take(fi)
            for dt_ in range(4):
                t1 = S.op("act", (lambda e, b=bset[dt_], fi=fi, dt_=dt_: e.activation(out=fT_sb[fi][:, dt_, :], in_=PS.t[b][:, :], func=AF.Copy)),
                          deps=[last[dt_], fdeps])
                PS.set(bset[dt_], t1)
                t_ev.append(t1)
            pend = (g, fi, t_ev)
        emit_transposes(*pend)

        for tb in range(TB):
            rows = slice(r0 + tb * 128, r0 + (tb + 1) * 128)
            t_r3 = emit_rstd(S, ss3[:, tb, :], D, rs[:, 2:3], tmpv[:, 2:3], epsb[:, 0:1], deps=[junk_t[0], t_eps])
            for pc in range(NPC):
                cs = slice(pc * PW, (pc + 1) * PW)
                i2 = pc % 2
                lm = S.dma("sp", rbM[i2][:, :], f_s[rows, cs], deps=[f_w[(tb, pc)], rbM_u.take(i2)])
                lx = S.dma("sp", rbX[i2][:, :], h1_s[rows, cs], deps=[rbX_u.take(i2), h1_written[(tt, tb)]])
                lg = S.dma("sp", rbG[i2][:, :], g_fpost[0, cs].partition_broadcast(128), deps=[rbG_u.take(i2)])
                ta = S.op("dve", (lambda e, i2=i2: e.scalar_tensor_tensor(out=rbM[i2][:, :], in0=rbM[i2][:, :], scalar=rs[:, 2:3],
                                                                           in1=rbG[i2][:, :], op0=ALU.mult, op1=ALU.mult)),
                          deps=[lm, lg, t_r3])
                rbG_u.add(i2, ta)
                to = S.op("dve", (lambda e, i2=i2: e.tensor_tensor(out=rbM[i2][:, :], in0=rbM[i2][:, :], in1=rbX[i2][:, :], op=ALU.add)),
                          deps=[ta, lx])
                rbX_u.add(i2, to)
                tw = S.dma("sp", out[rows, cs], rbM[i2][:, :], deps=[to])
                rbM_u.add(i2, tw)


def build_b():
    nc = bass.Bass("TRN2", target_bir_lowering=False)
    io = {}
    io["catT"] = nc.dram_tensor("catT", [D, TOK_PER_CORE], BF16, kind="ExternalInput").ap()
    io["xrows"] = nc.dram_tensor("xrows", [TOK_PER_CORE, D], F32, kind="ExternalInput").ap()
    io["w_out"] = nc.dram_tensor("w_out", [D, D], F32, kind="ExternalInput").ap()
    io["w_gate"] = nc.dram_tensor("w_gate", [D, FF], F32, kind="ExternalInput").ap()
    io["w_up"] = nc.dram_tensor("w_up", [D, FF], F32, kind="ExternalInput").ap()
    io["w_down"] = nc.dram_tensor("w_down", [FF, D], F32, kind="ExternalInput").ap()
    io["g_post"] = nc.dram_tensor("g_post", [1, D], F32, kind="ExternalInput").ap()
    io["g_fpost"] = nc.dram_tensor("g_fpost", [1, D], F32, kind="ExternalInput").ap()
    io["g2T"] = nc.dram_tensor("g2T", [128, KT], F32, kind="ExternalInput").ap()
    io["identb"] = nc.dram_tensor("identb", [128, 128], BF16, kind="ExternalInput").ap()
    io["identf"] = nc.dram_tensor("identf", [128, 128], F32, kind="ExternalInput").ap()
    io["out"] = nc.dram_tensor("out", [TOK_PER_CORE, D], F32, kind="ExternalOutput").ap()
    io["mixed_s"] = nc.dram_tensor("mixed_s", [TOK_PER_CORE, D], F32, kind="Internal").ap()
    io["f_s"] = nc.dram_tensor("f_s", [TOK_PER_CORE, D], F32, kind="Internal").ap()
    io["h1_s"] = nc.dram_tensor("h1_s", [TOK_PER_CORE, D], F32, kind="Internal").ap()
    io["wout_rt"] = cat_rowtile_map()
    with ExitStack() as stack:
        S = Sched(nc, stack)
        PS = Banks(nc, stack)
        AR = Arena(nc, stack)
        emit_phase_b(nc, S, stack, PS, io, AR)
        S.finalize()
        with nc.Block() as block:
            S.run(block)
    return nc


def cat_rowtile_map():
    m = []
    for r in range(NCORES):
        for j in range(4):
            if j < 2:
                m.append(r * 2 + j)
            else:
                m.append(16 + r * 2 + (j - 2))
    return m


def host_consts():
    identb = np.eye(128, dtype=np.float32).astype(ml_dtypes.bfloat16)
    identf = np.eye(128, dtype=np.float32)
    return identb, identf


def run_phase_b(catT_all, inputs, cores=None):
    nc = build_b()
    identb, identf = host_consts()
    x = np.ascontiguousarray(inputs["x"]).reshape(NBATCH * S_LEN, D)
    g2T = np.ascontiguousarray(np.asarray(inputs["ffn_pre_g"][0]).reshape(KT, 128).T)
    in_maps = []
    cores = list(range(NCORES)) if cores is None else cores
    for c in cores:
        sl = slice(c * TOK_PER_CORE, (c + 1) * TOK_PER_CORE)
        in_maps.append({
            "catT": np.ascontiguousarray(catT_all[:, sl]),
            "xrows": np.ascontiguousarray(x[sl]),
            "w_out": np.asarray(inputs["w_out"][0]),
            "w_gate": np.asarray(inputs["w_gate"][0]),
            "w_up": np.asarray(inputs["w_up"][0]),
            "w_down": np.asarray(inputs["w_down"][0]),
            "g_post": np.asarray(inputs["mix_post_g"]).reshape(1, D),
            "g_fpost": np.asarray(inputs["ffn_post_g"]).reshape(1, D),
            "g2T": g2T,
            "identb": identb,
            "identf": identf,
        })
    res = run_bass_kernel_spmd(nc, in_maps, core_ids=list(range(len(cores))))
    outs = [np.asarray(r["out"]) for r in res.results]
    return np.concatenate(outs, axis=0)


CH = 256
GELU_K = 2.0 * math.sqrt(2.0 / math.pi)


def emit_phase_a(nc, S, stack, PS, io, AR, fused=False):
    x_all, meta = io["x_all"], io["meta"]
    w_in = io["w_in_c"]
    cosT, sinT = io["cosT"], io["sinT"]
    cat_out = io["cat_out"]

    def sb(name, shape, dt):
        return AR.alloc(shape, dt)

    NTL = CH // 128
    Wb = sb("Wb", [128, KT, 1280], BF16)
    xa = sb("xa", [128, D], F32)
    xn = sb("xn", [128, D], BF16)
    uT = sb("uT", [128, KT, CH], BF16)
    KTs = sb("KTs", [128, 2, TPOS], BF16)
    Vs = sb("Vs", [128, 33, 257], BF16)
    QTs = sb("QTs", [128, 2, CH], BF16)
    cs_c = sb("cs_c", [128, CH], F32)
    sn_c = sb("sn_c", [128, CH], F32)
    ropeA = sb("ropeA", [128, CH], F32)
    ropeB = sb("ropeB", [128, CH], F32)
    XR = [sb(f"XR{i}", [128, 3 + CH], F32) for i in range(2)]
    GT = [sb(f"GT{i}", [128, CH], F32) for i in range(2)]
    LT = [{k: sb(f"lt_{k}{i}", [128, CH], F32) for k in ("xc", "t1", "t2", "r", "a", "om", "bt", "H", "g1", "g2")} for i in range(2)]
    xcb = [sb(f"xcb{i}", [128, CH], BF16) for i in range(2)]
    Eb = [sb(f"Eb{i}", [128, 2, 2, 128], BF16) for i in range(2)]
    Ob = sb("Ob", [128, 256], F32)
    ATb = sb("ATb", [128, 256], F32)
    Yb = sb("Yb", [128, 256], BF16)
    catS = [sb(f"catS{i}", [128, 4, CH], BF16) for i in range(2)]
    st = sb("stat", [128, 16], F32)
    state = sb("state", [128, 2], F32)
    state_m = sb("state_m", [128, 2], F32)
    hist_m = sb("hist_m", [128, 2, 3], F32)
    LC = sb("LC", [128, 2, 12], F32)
    lamb = sb("lamb", [128, 4, 128], F32)
    lamt = sb("lamt", [128, 8], F32)
    gsub = sb("gsub", [128, 256], F32)
    epsb = sb("epsA", [128, 1], F32)
    g1T = sb("g1T_sb", [128, KT], F32)
    identb = sb("identA", [128, 128], BF16)
    maskb = sb("maskb", [128, 128], BF16)
    WR = sb("WR", [128, 2, 128], BF16)
    WI = sb("WI", [128, 2, 128], BF16)

    wv = w_in.rearrange("(kt p) c -> p kt c", p=128)
    t_w = [S.dma("pool", Wb[:, q * 8:(q + 1) * 8, :], wv[:, q * 8:(q + 1) * 8, :]) for q in range(4)]
    t_wr = S.dma("pool", WR[:, :, :], io["w_r_c"].rearrange("h i j -> i h j"))
    t_wi = S.dma("pool", WI[:, :, :], io["w_i_c"].rearrange("h i j -> i h j"))
    t_g1 = S.dma("sp", g1T[:, :], io["g1T"][:, :])
    t_ib = S.dma("sp", identb[:, :], io["identb"][:, :])
    t_mk = S.dma("sp", maskb[:, :], io["mask"][:, :])
    t_lp = S.dma("sp", LC[:, :, 0:8], io["lru_p"][:, :, :])
    t_lv = S.dma("sp", lamb[:, :, :].rearrange("p a b -> p (a b)"), io["lamv"].rearrange("a b -> (a b)").partition_broadcast(128))
    t_sg = S.dma("sp", gsub[:, :], io["subg"][0, :].partition_broadcast(128))
    t_eps = S.op("dve", lambda e: e.memset(epsb[:, :], RMS_EPS))
    t_ones = S.op("dve", lambda e: e.memset(Vs[:, :, 256:257], 1.0))
    t_z1 = S.op("dve", lambda e: e.memset(state[:, :], 0.0))
    t_z2 = [S.op("dve", lambda e, i=i: e.memset(XR[i][:, 0:3], 0.0)) for i in range(2)]
    t = S.op("dve", lambda e: e.tensor_tensor(out=lamb[:, 0, :], in0=lamb[:, 0, :], in1=lamb[:, 1, :], op=ALU.mult), deps=[t_lv])
    t = S.op("dve", lambda e: e.tensor_tensor(out=lamb[:, 2, :], in0=lamb[:, 2, :], in1=lamb[:, 3, :], op=ALU.mult), deps=[t])
    t = S.op("dve", lambda e: e.tensor_reduce(out=lamt[:, 0:1], in_=lamb[:, 0, :], op=ALU.add, axis=AX.X), deps=[t])
    t = S.op("dve", lambda e: e.tensor_reduce(out=lamt[:, 1:2], in_=lamb[:, 2, :], op=ALU.add, axis=AX.X), deps=[t])
    t = S.op("act", lambda e: e.activation(out=lamt[:, 2:4], in_=lamt[:, 0:2], func=AF.Exp), deps=[t])
    t = S.op("dve", lambda e: e.tensor_tensor(out=lamt[:, 4:5], in0=lamt[:, 3:4], in1=lamt[:, 2:3], op=ALU.subtract), deps=[t])
    t_nlam = S.op("dve", lambda e: e.tensor_scalar(out=lamt[:, 4:5], in0=lamt[:, 4:5], scalar1=-LAMBDA_INIT, scalar2=None, op0=ALU.add), deps=[t])
    t_gs = S.op("dve", lambda e: e.tensor_scalar(out=gsub[:, :], in0=gsub[:, :], scalar1=1.0 - LAMBDA_INIT, scalar2=None, op0=ALU.mult), deps=[t_sg])
    t = S.op("act", lambda e: e.activation(out=LC[:, :, 10:11], in_=LC[:, :, 7:8], func=AF.Exp, scale=-1.0), deps=[t_lp])
    t = S.op("act", lambda e: e.activation(out=LC[:, :, 10:11], in_=LC[:, :, 10:11], func=AF.Ln, bias=1.0), deps=[t])
    t_c1 = S.op("dve", lambda e: e.tensor_scalar(out=LC[:, :, 7:8], in0=LC[:, :, 10:11], scalar1=-8.0, scalar2=None, op0=ALU.mult), deps=[t])
    t_c2 = S.op("dve", lambda e: e.tensor_scalar(out=LC[:, :, 8:9], in0=LC[:, :, 10:11], scalar1=-16.0, scalar2=None, op0=ALU.mult), deps=[t_c1])
    t_nb = S.op("dve", lambda e: e.tensor_scalar(out=LC[:, :, 5:7], in0=LC[:, :, 5:7], scalar1=-1.0, scalar2=None, op0=ALU.mult), deps=[t_lp])
    t_lc = [t_c1, t_c2, t_nb]

    xa_u, xn_u = Users(1), Users(1)
    XR_u, GT_u = Users(2), Users(2)
    cs_u = Users(1)
    rope_u = Users(1)
    E_u = Users(2)
    catS_u = Users(2)
    Y_u = Users(1)
    state_t = [[t_z1], [t_z1]]
    hist_t = [[t_z2[0]], [t_z2[1]]]
    proj_rot = [0]
    misc_rot = [0]
    pend_store = [None]
    chunk_no = [0]

    def proj_bank():
        b = (0, 1)[proj_rot[0] % 2]
        proj_rot[0] += 1
        return b

    def misc_bank():
        b = (2, 7)[misc_rot[0] % 2]
        misc_rot[0] += 1
        return b

    def cat_store(pcur, pcol, pdeps):
        if not fused:
            return [S.dma("sp", cat_out.rearrange("(t p) n -> p t n", p=128)[:, :, pcol:pcol + CH], catS[pcur][:, :, :], deps=pdeps)]
        j_ = pcol // TOK_PER_CORE
        w0 = pcol % TOK_PER_CORE
        toks = []
        for half, dst in enumerate((io["snd_attn"], io["snd_rec"])):
            dv = dst.rearrange("(t j p) w -> p t j w", t=2, j=NCORES, p=128)
            toks.append(S.dma("sp", dv[:, :, j_, w0:w0 + CH], catS[pcur][:, half * 2:half * 2 + 2, :], deps=pdeps))
        return toks

    COL = {"q0": 0, "q1": 128, "k0": 256, "k1": 384, "v": 512, "xr0": 768, "xr1": 896, "g0": 1024, "g1": 1152}

    def step1(kind, b, ci):
        is_x = kind == "x"
        tiles = [(tl, 128) for tl in range(NTL)] if is_x else [(0, NMETA)]
        uT_w = []
        for tl, np_ in tiles:
            if is_x:
                r0 = b * S_LEN + ci * CH + tl * 128
                src = x_all[r0:r0 + np_, :]
            else:
                src = meta[0:np_, :]
            t_x = S.dma("sp", xa[0:np_, :], src, deps=xa_u.take(0))
            t_sq = S.op("act", lambda e, np_=np_: e.activation(out=xn[0:np_, :], in_=xa[0:np_, :], func=AF.Square, accum_out=st[0:np_, 0:1]),
                        deps=[t_x, xn_u.take(0)])
            t1 = S.op("act", lambda e, np_=np_: e.activation(out=st[0:np_, 1:2], in_=st[0:np_, 0:1], func=AF.Ln, bias=epsb[0:np_, 0:1], scale=1.0 / D),
                      deps=[t_sq, t_eps])
            t2 = S.op("act", lambda e, np_=np_: e.activation(out=st[0:np_, 2:3], in_=st[0:np_, 1:2], func=AF.Exp, scale=-0.5), deps=[t1])
            t_xn = S.op("act", lambda e, np_=np_: e.activation(out=xn[0:np_, :], in_=xa[0:np_, :], func=AF.Copy, scale=st[0:np_, 2:3]), deps=[t2])
            xa_u.add(0, t_xn)
            for q in range(4):
                bk = misc_bank()
                pv = PS.t[bk][:, :].bitcast(BF16).rearrange("p (a t) -> p a t", a=8)
                tp = None
                for j in range(8):
                    kt = q * 8 + j
                    tp = S.op("pe", lambda e, np_=np_, kt=kt, j=j, pv=pv: e.transpose(out=pv[:, j, 0:np_], in_=xn[0:np_, kt * 128:(kt + 1) * 128],
                                                                                 identity=identb[0:np_, 0:np_]),
                              deps=[t_xn, t_ib, PS.dep(bk) if j == 0 else None], signal=(j == 7))
                te = S.op("dve", lambda e, np_=np_, q=q, pv=pv, tl=tl: e.tensor_tensor(
                    out=uT[:, q * 8:(q + 1) * 8, tl * 128:tl * 128 + np_], in0=pv[:, :, 0:np_],
                    in1=g1T[:, q * 8:(q + 1) * 8].unsqueeze(2).to_broadcast([128, 8, np_]), op=ALU.mult),
                          deps=[tp, t_g1])
                PS.set(bk, te)
                uT_w.append(te)
            xn_u.add(0, tp)
        return uT_w

    def process_chunk(kind, b, ci, uT_w, nxt):
        is_x = kind == "x"
        n = CH if is_x else NMETA
        pos0 = NMETA + ci * CH if is_x else 0
        tiles = [(tl, 128) for tl in range(NTL)] if is_x else [(0, NMETA)]
        cur = chunk_no[0] % 2
        chunk_no[0] += 1
        cdeps = cs_u.take(0)
        t_cs = S.dma("sp", cs_c[:, 0:n], cosT[:, pos0:pos0 + n], deps=cdeps)
        t_sn = S.dma("sp", sn_c[:, 0:n], sinT[:, pos0:pos0 + n], deps=cdeps)
        names = ["k0", "k1"] + (["q0", "q1"] if is_x else []) + ["xr0", "xr1"] + (["g0", "g1"] if is_x else [])
        t_q = [None, None]
        t_k = [None, None]
        t_xr = [None, None]
        t_g = [None, None]
        for nm in names:
            bk = proj_bank()
            col = COL[nm]
            tk = None
            for kt in range(KT):
                tk = S.op("pe", lambda e, bk=bk, kt=kt, col=col: e.matmul(PS.t[bk][:, 0:n], lhsT=Wb[:, kt, col:col + 128], rhs=uT[:, kt, 0:n],
                                                                        start=(kt == 0), stop=(kt == KT - 1)),
                          deps=[uT_w, t_w, PS.dep(bk)] if kt == 0 else [], signal=(kt == KT - 1))
            PS.set(bk, tk)
            P = PS.t[bk]
            if nm[0] in "qk":
                comp = int(nm[1])
                dst = (QTs[:, comp, 0:n] if nm[0] == "q" else KTs[:, comp, pos0:pos0 + n])
                rd = rope_u.take(0)
                ta = S.op("dve", lambda e, P=P: e.tensor_tensor(out=ropeA[:, 0:n], in0=P[:, 0:n], in1=cs_c[:, 0:n], op=ALU.mult),
                          deps=[tk, t_cs, rd])
                tb1 = S.op("dve", lambda e, P=P: e.tensor_tensor(out=ropeB[0:64, 0:n], in0=P[64:128, 0:n], in1=sn_c[64:128, 0:n], op=ALU.mult),
                           deps=[ta, t_sn])
                tb_ = S.op("dve", lambda e, P=P: e.tensor_tensor(out=ropeB[64:128, 0:n], in0=P[0:64, 0:n], in1=sn_c[0:64, 0:n], op=ALU.mult),
                           deps=[tb1])
                PS.set(bk, tb_)
                th_ = S.op("dve", lambda e, dst=dst: e.tensor_tensor(out=dst, in0=ropeA[:, 0:n], in1=ropeB[:, 0:n], op=ALU.add),
                           deps=[tb_])
                rope_u.add(0, th_)
                cs_u.add(0, th_)
                if nm[0] == "q":
                    t_q[comp] = th_
                else:
                    t_k[comp] = th_
            elif nm[0] == "x":
                ct = int(nm[2])
                te = S.op("act", lambda e, P=P, ct=ct: e.activation(out=XR[ct][:, 3:3 + n], in_=P[:, 0:n], func=AF.Copy),
                          deps=[tk, XR_u.take(ct)])
                PS.set(bk, te)
                t_xr[ct] = te
            else:
                ct = int(nm[1])
                te = S.op("act", lambda e, P=P, ct=ct: e.activation(out=GT[ct][:, 0:n], in_=P[:, 0:n], func=AF.Copy),
                          deps=[tk, GT_u.take(ct)])
                PS.set(bk, te)
                t_g[ct] = te
        t_v = {}
        for tl, np_ in tiles:
            bk = misc_bank()
            vblk = (1 + ci * NTL + tl) if is_x else 0
            tk = None
            for kt in range(KT):
                tk = S.op("pe", lambda e, bk=bk, kt=kt, tl=tl, np_=np_: e.matmul(PS.t[bk][0:np_, 0:256], lhsT=uT[:, kt, tl * 128:tl * 128 + np_],
                                                                               rhs=Wb[:, kt, 512:768], start=(kt == 0), stop=(kt == KT - 1)),
                          deps=[uT_w, t_w, PS.dep(bk)] if kt == 0 else [], signal=(kt == KT - 1))
            te = S.op("act", lambda e, bk=bk, np_=np_, vblk=vblk: e.activation(out=Vs[0:np_, vblk, 0:256], in_=PS.t[bk][0:np_, 0:256], func=AF.Copy),
                      deps=[tk, t_ones])
            PS.set(bk, te)
            t_v[vblk] = te
            vready[vblk] = te
        if not is_x:
            kready["meta"] = t_k
        else:
            for tl in range(NTL):
                kready[ci * NTL + tl] = t_k
        uT_next = step1(*nxt) if nxt is not None else None
        if is_x:
            for ql in range(NTL):
                gq = ci * NTL + ql
                entries = [("meta", NMETA, 0, 0)] + [(kb, 128, NMETA + kb * 128, 1 + kb) for kb in range(gq + 1)]
                groups = [[entries[0]]] + [entries[i:i + 2] for i in range(1, len(entries), 2)]
                ng = len(groups)
                s_tok = [None] * ng
                e_tok = [None] * ng

                def emit_S(gi):
                    grp = groups[gi]
                    sbk = (3, 4)[gi % 2]
                    npg = grp[0][1]
                    bv = PS.t[sbk][:, :].rearrange("p (c l q) -> p c l q", c=2, l=2)
                    tk = None
                    first = True
                    for comp in range(2):
                        for li, (kb, np_, kpos, vblk) in enumerate(grp):
                            tk = S.op("pe", lambda e, bv=bv, comp=comp, li=li, np_=np_, kpos=kpos, ql=ql: e.matmul(
                                bv[0:np_, comp, li, :], lhsT=KTs[:, comp, kpos:kpos + np_], rhs=QTs[:, comp, ql * 128:(ql + 1) * 128],
                                start=True, stop=True),
                                      deps=[PS.dep(sbk), t_q, kready[kb], E_u.take(gi % 2)] if first else [kready[kb]],
                                      signal=(comp == 1 and li == len(grp) - 1))
                            first = False
                    PS.set(sbk, tk)
                    nl = len(grp)
                    ev = Eb[gi % 2]
                    te = S.op("act", lambda e, bv=bv, ev=ev, npg=npg, nl=nl: e.activation(out=ev[0:npg, :, 0:nl, :], in_=bv[0:npg, :, 0:nl, :],
                                                                                     func=AF.Exp, scale=ATT_SCALE),
                              deps=[tk])
                    PS.set(sbk, te)
                    if grp[-1][0] == gq:
                        li = nl - 1
                        te = S.op("pool", lambda e, ev=ev, li=li: e.tensor_tensor(out=ev[:, :, li, :], in0=ev[:, :, li, :],
                                                                                  in1=maskb[:, :].unsqueeze(1).to_broadcast([128, 2, 128]), op=ALU.mult),
                                  deps=[te, t_mk])
                    e_tok[gi] = te

                def emit_PV(gi):
                    grp = groups[gi]
                    ev = Eb[gi % 2]
                    tk = None
                    for comp in range(2):
                        abk = 5 + comp
                        for li, (kb, np_, kpos, vblk) in enumerate(grp):
                            first = (gi == 0 and li == 0)
                            last = (gi == ng - 1 and li == len(grp) - 1)
                            tk = S.op("pe", lambda e, abk=abk, ev=ev, comp=comp, li=li, np_=np_, vblk=vblk, first=first, last=last: e.matmul(
                                PS.t[abk][:, 0:257], lhsT=ev[0:np_, comp, li, :], rhs=Vs[0:np_, vblk, :], start=first, stop=last),
                                      deps=[e_tok[gi], vready[vblk], PS.dep(abk) if first else None],
                                      signal=(last or (comp == 1 and li == len(grp) - 1)))
                            if last:
                                PS.set(abk, tk)
                    E_u.add(gi % 2, tk)

                emit_S(0)
                for gi in range(ng):
                    if gi + 1 < ng:
                        emit_S(gi + 1)
                    emit_PV(gi)
                a0, a1 = PS.t[5], PS.t[6]
                tr0 = S.op("dve", lambda e: e.reciprocal(out=st[:, 3:4], in_=a0[:, 256:257]), deps=[PS.dep(5)])
                tr1 = S.op("dve", lambda e: e.reciprocal(out=st[:, 4:5], in_=a1[:, 256:257]), deps=[PS.dep(6)])
                tr1 = S.op("dve", lambda e: e.tensor_tensor(out=st[:, 4:5], in0=st[:, 4:5], in1=lamt[:, 4:5], op=ALU.mult), deps=[tr1, t_nlam])
                to = S.op("act", lambda e: e.activation(out=Ob[:, :], in_=a0[:, 0:256], func=AF.Copy, scale=st[:, 3:4]), deps=[tr0])
                PS.set(5, to)
                tat = S.op("dve", lambda e: e.scalar_tensor_tensor(out=ATb[:, :], in0=a1[:, 0:256], scalar=st[:, 4:5], in1=Ob[:, :],
                                                                   op0=ALU.mult, op1=ALU.add), deps=[to, tr1])
                PS.set(6, tat)
                tsq = S.op("act", lambda e: e.activation(out=Yb[:, :], in_=ATb[:, :], func=AF.Square, accum_out=st[:, 5:6]),
                           deps=[tat, Y_u.take(0)])
                t1 = S.op("act", lambda e: e.activation(out=st[:, 6:7], in_=st[:, 5:6], func=AF.Ln, bias=epsb[:, 0:1], scale=1.0 / 256), deps=[tsq])
                t2 = S.op("act", lambda e: e.activation(out=st[:, 7:8], in_=st[:, 6:7], func=AF.Exp, scale=-0.5), deps=[t1])
                ty = S.op("dve", lambda e: e.scalar_tensor_tensor(out=Yb[:, :], in0=ATb[:, :], scalar=st[:, 7:8], in1=gsub[:, :],
                                                                  op0=ALU.mult, op1=ALU.mult), deps=[t2, t_gs])
                bk = misc_bank()
                pv = PS.t[bk][:, 0:128].bitcast(BF16).rearrange("p (a t) -> p a t", a=2)
                tp = None
                for h in range(2):
                    tp = S.op("pe", lambda e, h=h, pv=pv: e.transpose(out=pv[:, h, :], in_=Yb[:, h * 128:(h + 1) * 128], identity=identb[:, :]),
                              deps=[ty, PS.dep(bk) if h == 0 else None], signal=(h == 1))
                Y_u.add(0, tp)
                tc_ = S.op("act", lambda e, pv=pv, ql=ql: e.activation(out=catS[cur][:, 0:2, ql * 128:(ql + 1) * 128], in_=pv, func=AF.Copy),
                           deps=[tp, catS_u.take(cur) if ql == 0 else None])
                PS.set(bk, tc_)
                cat_parts.append(tc_)
        for ct in range(2):
            L = LT[ct]
            cw = lambda k, ct=ct: LC[:, ct, k:k + 1]
            xr = XR[ct]
            xc = L["xc"]
            t = S.op("dve", lambda e, xr=xr, xc=xc, ct=ct: e.tensor_scalar(out=xc[:, 0:n], in0=xr[:, 3:3 + n], scalar1=LC[:, ct, 3:4], scalar2=LC[:, ct, 4:5],
                                                                      op0=ALU.mult, op1=ALU.add),
                     deps=[t_xr[ct], hist_t[ct], t_lp])
            for w in (2, 1, 0):
                t = S.op("dve", lambda e, xr=xr, xc=xc, ct=ct, w=w: e.scalar_tensor_tensor(out=xc[:, 0:n], in0=xr[:, w:w + n], scalar=LC[:, ct, w:w + 1],
                                                                                       in1=xc[:, 0:n], op0=ALU.mult, op1=ALU.add), deps=[t])
            t_xc = t
            th = S.op("pool", lambda e, xr=xr: e.tensor_copy(out=xr[:, 0:3], in_=xr[:, n:n + 3]), deps=[t_xc])
            hist_t[ct] = [th]
            XR_u.add(ct, th)
            tb16 = S.op("act", lambda e, xc=xc, ct=ct: e.activation(out=xcb[ct][:, 0:n], in_=xc[:, 0:n], func=AF.Copy), deps=[t_xc])
            bk = misc_bank()
            P = PS.t[bk]
            t_mr = S.op("pe", lambda e, P=P, ct=ct: e.matmul(P[:, 0:n], lhsT=WR[:, ct, :], rhs=xcb[ct][:, 0:n], start=True, stop=True),
                        deps=[tb16, t_wr, PS.dep(bk)], signal=False)
            t_mi = S.op("pe", lambda e, P=P, ct=ct: e.matmul(P[:, 256:256 + n], lhsT=WI[:, ct, :], rhs=xcb[ct][:, 0:n], start=True, stop=True),
                        deps=[t_wi], signal=True)
            PS.set(bk, t_mi)
            t = S.op("act", lambda e, P=P, L=L, ct=ct: e.activation(out=L["t1"][:, 0:n], in_=P[:, 0:n], func=AF.Exp, scale=-1.0, bias=LC[:, ct, 5:6]),
                     deps=[t_mi, t_lc])
            t = S.op("act", lambda e, L=L: e.activation(out=L["t1"][:, 0:n], in_=L["t1"][:, 0:n], func=AF.Ln, bias=1.0), deps=[t])
            t_r = S.op("act", lambda e, L=L: e.activation(out=L["r"][:, 0:n], in_=L["t1"][:, 0:n], func=AF.Exp, scale=-1.0), deps=[t])
            t = S.op("act", lambda e, P=P, L=L, ct=ct: e.activation(out=L["t2"][:, 0:n], in_=P[:, 256:256 + n], func=AF.Exp, scale=-1.0, bias=LC[:, ct, 6:7]),
                     deps=[t_mi, t_lc])
            PS.set(bk, t)
            t_spi = S.op("act", lambda e, L=L: e.activation(out=L["t2"][:, 0:n], in_=L["t2"][:, 0:n], func=AF.Ln, bias=1.0), deps=[t])
            t_a = S.op("act", lambda e, L=L, ct=ct: e.activation(out=L["a"][:, 0:n], in_=L["r"][:, 0:n], func=AF.Exp, scale=LC[:, ct, 7:8]), deps=[t_r, t_lc])
            t = S.op("act", lambda e, L=L, ct=ct: e.activation(out=L["om"][:, 0:n], in_=L["r"][:, 0:n], func=AF.Exp, scale=LC[:, ct, 8:9]), deps=[t_r, t_lc])
            t = S.op("pool", lambda e, L=L: e.tensor_scalar(out=L["om"][:, 0:n], in0=L["om"][:, 0:n], scalar1=-1.0, scalar2=1.0, op0=ALU.mult, op1=ALU.add), deps=[t])
            t = S.op("act", lambda e, L=L: e.activation(out=L["om"][:, 0:n], in_=L["om"][:, 0:n], func=AF.Ln), deps=[t])
            t = S.op("dve", lambda e, L=L: e.scalar_tensor_tensor(out=L["om"][:, 0:n], in0=L["om"][:, 0:n], scalar=0.5, in1=L["t2"][:, 0:n],
                                                                   op0=ALU.mult, op1=ALU.subtract), deps=[t, t_spi])
            t = S.op("act", lambda e, L=L: e.activation(out=L["om"][:, 0:n], in_=L["om"][:, 0:n], func=AF.Exp), deps=[t])
            t_bt = S.op("pool", lambda e, L=L, xc=xc: e.tensor_tensor(out=L["bt"][:, 0:n], in0=L["om"][:, 0:n], in1=xc[:, 0:n], op=ALU.mult), deps=[t, t_xc])
            t_H = S.op("dve", lambda e, L=L, ct=ct: e.tensor_tensor_scan(out=L["H"][:, 0:n], data0=L["a"][:, 0:n], data1=L["bt"][:, 0:n],
                                                                      initial=state[:, ct:ct + 1], op0=ALU.mult, op1=ALU.add),
                       deps=[t_a, t_bt, state_t[ct]])
            t_st = S.op("act", lambda e, L=L, ct=ct: e.activation(out=state[:, ct:ct + 1], in_=L["H"][:, n - 1:n], func=AF.Copy), deps=[t_H])
            state_t[ct] = [t_st]
            if is_x:
                gt = GT[ct]
                t = S.op("pool", lambda e, L=L, gt=gt: e.tensor_tensor(out=L["g1"][:, 0:n], in0=gt[:, 0:n], in1=gt[:, 0:n], op=ALU.mult), deps=[t_g[ct]])
                t = S.op("pool", lambda e, L=L: e.tensor_scalar(out=L["g1"][:, 0:n], in0=L["g1"][:, 0:n], scalar1=0.044715, scalar2=1.0,
                                                                op0=ALU.mult, op1=ALU.add), deps=[t])
                t = S.op("pool", lambda e, L=L, gt=gt: e.tensor_tensor(out=L["g1"][:, 0:n], in0=L["g1"][:, 0:n], in1=gt[:, 0:n], op=ALU.mult), deps=[t])
                t = S.op("act", lambda e, L=L: e.activation(out=L["g2"][:, 0:n], in_=L["g1"][:, 0:n], func=AF.Exp, scale=-GELU_K), deps=[t])
                t = S.op("act", lambda e, L=L: e.activation(out=L["g2"][:, 0:n], in_=L["g2"][:, 0:n], func=AF.Ln, bias=1.0), deps=[t])
                t = S.op("act", lambda e, L=L: e.activation(out=L["g2"][:, 0:n], in_=L["g2"][:, 0:n], func=AF.Exp, scale=-1.0), deps=[t])
                t = S.op("pool", lambda e, L=L, gt=gt: e.tensor_tensor(out=L["g2"][:, 0:n], in0=L["g2"][:, 0:n], in1=gt[:, 0:n], op=ALU.mult), deps=[t])
                GT_u.add(ct, t)
                t = S.op("pool", lambda e, L=L, ct=ct: e.tensor_tensor(out=catS[cur][:, 2 + ct, 0:n], in0=L["H"][:, 0:n], in1=L["g2"][:, 0:n], op=ALU.mult),
                         deps=[t, t_H, catS_u.take(cur)])
                cat_parts.append(t)
        if pend_store[0] is not None:
            pcur, pcol, pdeps = pend_store[0]
            for tw in cat_store(pcur, pcol, pdeps):
                catS_u.add(pcur, tw)
            pend_store[0] = None
        if is_x:
            pend_store[0] = (cur, b * S_LEN + ci * CH, list(cat_parts))
            del cat_parts[:]
        return uT_next

    vready = {}
    kready = {}
    cat_parts = []
    chunks = [("x", b_, ci_) for b_ in range(NBATCH) for ci_ in range(S_LEN // CH)]
    uTw = step1("meta", 0, 0)
    uTw = process_chunk("meta", 0, 0, uTw, chunks[0])
    t_sm = S.op("dve", lambda e: e.tensor_copy(out=state_m[:, :], in_=state[:, :]), deps=[state_t[0], state_t[1]])
    t_hm = [S.op("dve", lambda e, i=i: e.tensor_copy(out=hist_m[:, i, :], in_=XR[i][:, 0:3]), deps=[hist_t[i]]) for i in range(2)]
    for b in range(NBATCH):
        if b > 0:
            t_rs = S.op("dve", lambda e: e.tensor_copy(out=state[:, :], in_=state_m[:, :]), deps=[t_sm, state_t[0], state_t[1]])
            state_t[0] = [t_rs]
            state_t[1] = [t_rs]
            for i in range(2):
                t_rh = S.op("dve", lambda e, i=i: e.tensor_copy(out=XR[i][:, 0:3], in_=hist_m[:, i, :]), deps=[t_hm[i], hist_t[i], XR_u.take(i)])
                hist_t[i] = [t_rh]
        for ci in range(S_LEN // CH):
            idx = b * (S_LEN // CH) + ci
            uTw = process_chunk("x", b, ci, uTw, chunks[idx + 1] if idx + 1 < len(chunks) else None)
    pcur, pcol, pdeps = pend_store[0]
    cat_store(pcur, pcol, pdeps)


def build_a():
    nc = bass.Bass("TRN2", target_bir_lowering=False)
    io = {}
    def inp(name, shape, dt):
        io[name] = nc.dram_tensor(name, shape, dt, kind="ExternalInput").ap()
    inp("x_all", [NBATCH * S_LEN, D], F32)
    inp("meta", [NMETA, D], F32)
    inp("w_in_c", [D, 1280], F32)
    inp("g1T", [128, KT], F32)
    inp("cosT", [128, TPOS], F32)
    inp("sinT", [128, TPOS], F32)
    inp("lamv", [4, 128], F32)
    inp("subg", [1, 256], F32)
    inp("lru_p", [128, 2, 8], F32)
    inp("w_r_c", [2, 128, 128], F32)
    inp("w_i_c", [2, 128, 128], F32)
    inp("mask", [128, 128], BF16)
    inp("identb", [128, 128], BF16)
    io["cat_out"] = nc.dram_tensor("cat_out", [512, NBATCH * S_LEN], BF16, kind="ExternalOutput").ap()
    with ExitStack() as stack:
        S = Sched(nc, stack)
        PS = Banks(nc, stack)
        AR = Arena(nc, stack)
        emit_phase_a(nc, S, stack, PS, io, AR)
        S.finalize()
        with nc.Block() as block:
            S.run(block)
    return nc


def rope_tables():
    inv_freq = (1.0 / (np.float32(10000.0) ** (np.arange(0, 128, 2, dtype=np.float32) / np.float32(128)))).astype(np.float32)
    ang = np.arange(TPOS, dtype=np.float32)[:, None] * inv_freq[None, :]
    cos = np.cos(ang).astype(np.float32).T
    sin = np.sin(ang).astype(np.float32).T
    return (np.ascontiguousarray(np.concatenate([cos, cos], axis=0)), np.ascontiguousarray(np.concatenate([sin, -sin], axis=0)))


def phase_a_inputs(inputs, cores):
    identb, _ = host_consts()
    mask = np.triu(np.ones((128, 128), np.float32)).astype(ml_dtypes.bfloat16)
    cosT, sinT = rope_tables()
    x = np.ascontiguousarray(np.asarray(inputs["x"])).reshape(NBATCH * S_LEN, D)
    w_in = np.asarray(inputs["w_in"][0])
    g1T = np.ascontiguousarray(np.asarray(inputs["mix_pre_g"][0]).reshape(KT, 128).T)
    lamv = np.stack([np.asarray(inputs[k][0]) for k in ("lambda_q1", "lambda_k1", "lambda_q2", "lambda_k2")]).astype(np.float32)
    maps = []
    for c in cores:
        cols = np.concatenate([np.arange(c * 256, (c + 1) * 256) + off for off in (0, 2048, 4096, 6144, 8192)])
        ch = slice(c * 256, (c + 1) * 256)
        lp = np.stack([np.asarray(inputs["conv_w"][0])[0, ch], np.asarray(inputs["conv_w"][0])[1, ch], np.asarray(inputs["conv_w"][0])[2, ch],
                       np.asarray(inputs["conv_w"][0])[3, ch], np.asarray(inputs["conv_b"][0])[ch], np.asarray(inputs["b_r"][0])[ch],
                       np.asarray(inputs["b_i"][0])[ch], np.asarray(inputs["lru_lambda"][0])[ch]], axis=-1)
        lp = np.ascontiguousarray(lp.reshape(2, 128, 8).transpose(1, 0, 2)).astype(np.float32)
        maps.append({
            "x_all": x,
            "meta": np.asarray(inputs["meta_tokens"]),
            "w_in_c": np.ascontiguousarray(w_in[:, cols]),
            "g1T": g1T,
            "cosT": cosT,
            "sinT": sinT,
            "lamv": lamv,
            "subg": np.asarray(inputs["subln_g"]).reshape(1, 256),
            "lru_p": lp,
            "w_r_c": np.ascontiguousarray(np.asarray(inputs["w_r"][0])[2 * c:2 * c + 2]),
            "w_i_c": np.ascontiguousarray(np.asarray(inputs["w_i"][0])[2 * c:2 * c + 2]),
            "mask": mask,
            "identb": identb,
        })
    return maps


def run_phase_a(inputs, cores=None):
    cores = list(range(NCORES)) if cores is None else cores
    nc = build_a()
    maps = phase_a_inputs(inputs, cores)
    res = run_bass_kernel_spmd(nc, maps, core_ids=list(range(len(cores))))
    return [np.asarray(r["cat_out"]) for r in res.results]


def build_fused():
    nc = bass.Bass("TRN2", target_bir_lowering=False)
    io = {}

    def inp(name, shape, dt):
        io[name] = nc.dram_tensor(name, shape, dt, kind="ExternalInput").ap()

    inp("x_all", [NBATCH * S_LEN, D], F32)
    inp("meta", [NMETA, D], F32)
    inp("w_in_c", [D, 1280], F32)
    inp("g1T", [128, KT], F32)
    inp("cosT", [128, TPOS], F32)
    inp("sinT", [128, TPOS], F32)
    inp("lamv", [4, 128], F32)
    inp("subg", [1, 256], F32)
    inp("lru_p", [128, 2, 8], F32)
    inp("w_r_c", [2, 128, 128], F32)
    inp("w_i_c", [2, 128, 128], F32)
    inp("mask", [128, 128], BF16)
    inp("identb", [128, 128], BF16)
    inp("identf", [128, 128], F32)
    inp("xrows", [TOK_PER_CORE, D], F32)
    inp("w_out", [D, D], F32)
    inp("w_gate", [D, FF], F32)
    inp("w_up", [D, FF], F32)
    inp("w_down", [FF, D], F32)
    inp("g_post", [1, D], F32)
    inp("g_fpost", [1, D], F32)
    inp("g2T", [128, KT], F32)
    io["out"] = nc.dram_tensor("out", [TOK_PER_CORE, D], F32, kind="ExternalOutput").ap()
    io["mixed_s"] = nc.dram_tensor("mixed_s", [TOK_PER_CORE, D], F32).ap()
    io["f_s"] = nc.dram_tensor("f_s", [TOK_PER_CORE, D], F32).ap()
    io["h1_s"] = nc.dram_tensor("h1_s", [TOK_PER_CORE, D], F32).ap()
    snd_attn = nc.dram_tensor("snd_attn", [NCORES * 256, TOK_PER_CORE], BF16)
    snd_rec = nc.dram_tensor("snd_rec", [NCORES * 256, TOK_PER_CORE], BF16)
    ag_attn = nc.dram_tensor("ag_attn", [NCORES * NCORES * 256, TOK_PER_CORE], BF16)
    ag_rec = nc.dram_tensor("ag_rec", [NCORES * NCORES * 256, TOK_PER_CORE], BF16)
    io["snd_attn"], io["snd_rec"] = snd_attn.ap(), snd_rec.ap()
    io["ag_attn"], io["ag_rec"] = ag_attn.ap(), ag_rec.ap()
    io["catT"] = None
    io["cat_out"] = None
    io["wout_rt"] = None
    with ExitStack() as stack:
        S = Sched(nc, stack)
        PS = Banks(nc, stack)
        AR = Arena(nc, stack)
        emit_phase_a(nc, S, stack, PS, io, AR, fused=True)
        done_a = S.all_done_tokens()
        grp = [list(range(NCORES))]
        t_ag1 = S.collective(lambda e: e.collective_compute("AllGather", ALU.bypass, replica_groups=grp,
                                                            ins=[snd_attn.ap().opt()], outs=[ag_attn.ap().opt()]), deps=done_a)
        t_ag2 = S.collective(lambda e: e.collective_compute("AllGather", ALU.bypass, replica_groups=grp,
                                                            ins=[snd_rec.ap().opt()], outs=[ag_rec.ap().opt()]), deps=[t_ag1])
        S.barrier(done_a)
        AR.reset()
        emit_phase_b(nc, S, stack, PS, io, AR, cat_ready=[t_ag1, t_ag2], fused=True)
        S.finalize()
        with nc.Block() as block:
            S.run(block)
    return nc


_NC_CACHE = {}


def kernel(**inputs):
    if "nc" not in _NC_CACHE:
        _NC_CACHE["nc"] = build_fused()
    nc = _NC_CACHE["nc"]
    cores = list(range(NCORES))
    maps = phase_a_inputs(inputs, cores)
    _, identf = host_consts()
    x = np.ascontiguousarray(np.asarray(inputs["x"])).reshape(NBATCH * S_LEN, D)
    g2T = np.ascontiguousarray(np.asarray(inputs["ffn_pre_g"][0]).reshape(KT, 128).T)
    for c in cores:
        sl = slice(c * TOK_PER_CORE, (c + 1) * TOK_PER_CORE)
        maps[c].update({
            "identf": identf,
            "xrows": x[sl],
            "w_out": np.asarray(inputs["w_out"][0]),
            "w_gate": np.asarray(inputs["w_gate"][0]),
            "w_up": np.asarray(inputs["w_up"][0]),
            "w_down": np.asarray(inputs["w_down"][0]),
            "g_post": np.asarray(inputs["mix_post_g"]).reshape(1, D),
            "g_fpost": np.asarray(inputs["ffn_post_g"]).reshape(1, D),
            "g2T": g2T,
        })
    res = run_bass_kernel_spmd(nc, maps, core_ids=cores)
    outs = [np.asarray(r["out"]) for r in res.results]
    return np.concatenate(outs, axis=0).reshape(NBATCH, S_LEN, D).astype(np.float32)
```

```python
import math
from contextlib import ExitStack

import numpy as np
import ml_dtypes

import concourse.bass as bass
import concourse.mybir as mybir
from concourse.bass_utils import run_bass_kernel_spmd

F32 = mybir.dt.float32
BF16 = mybir.dt.bfloat16
AF = mybir.ActivationFunctionType
ALU = mybir.AluOpType
AX = mybir.AxisListType

NCORES = 8
D = 4096
KT = D // 128
FF = 11008
FT = FF // 128
S_LEN = 4096
NBATCH = 2
NMETA = 16
TPOS = NMETA + S_LEN
TOK_PER_CORE = 1024
RMS_EPS = 1e-6
LAMBDA_INIT = 0.8 - 0.6 * math.exp(-0.3 * 0)
ATT_SCALE = 128 ** -0.5


def _is_tok(d):
    return isinstance(d, tuple) and len(d) == 2 and isinstance(d[1], int) and isinstance(d[0], tuple)


def _flat(deps, out):
    for d in deps:
        if d is None:
            continue
        if _is_tok(d):
            out.append(d)
        else:
            _flat(d, out)
    return out


class Sched:
    ENG = ("pe", "act", "dve", "pool", "sp")

    def __init__(self, nc, stack, ndma=8):
        self.nc = nc
        self.q = {e: [] for e in self.ENG}
        self.sems = {}
        for e in ("pe", "act", "dve", "pool"):
            self.sems[("e", e)] = stack.enter_context(nc.semaphore(f"es_{e}"))
        self.ecnt = {e: 0 for e in ("pe", "act", "dve", "pool")}
        self.ndma = ndma
        self.dcnt = {}
        self.dnext = {}
        for e in ("sp", "pool", "act"):
            for i in range(ndma):
                self.sems[("d", e, i)] = stack.enter_context(nc.semaphore(f"ds_{e}{i}"))
                self.dcnt[(e, i)] = 0
            self.dnext[e] = 0
        self.waited = {e: {} for e in self.ENG}
        self.need_pid = set()
        self.pid = {}
        self.sems[("cc",)] = stack.enter_context(nc.semaphore("cc_sem"))
        self.cc_cnt = 0

    def collective(self, fn, deps=()):
        waits = self._waits("pool", deps)
        self.cc_cnt += 1
        self.q["pool"].append((waits, fn, ("cc",), 1))
        return (("cc",), self.cc_cnt)

    def all_done_tokens(self):
        toks = [(("e", e), self.ecnt[e]) for e in self.ecnt if self.ecnt[e] > 0]
        toks += [(("d", e, i), self.dcnt[(e, i)]) for (e, i) in self.dcnt if self.dcnt[(e, i)] > 0]
        if self.cc_cnt:
            toks.append((("cc",), self.cc_cnt))
        return toks

    def barrier(self, toks):
        for e in self.ENG:
            waits = self._waits(e, toks)
            if waits:
                self.q[e].append((waits, None, None, 0))

    def _waits(self, eng, deps):
        out = []
        for key, val in _flat(deps, []):
            if key == ("e", eng) and eng == "pe":
                continue
            if self.waited[eng].get(key, 0) >= val:
                continue
            self.waited[eng][key] = val
            out.append((key, val))
        return out

    def op(self, eng, fn, deps=(), signal=None):
        if signal is None:
            signal = eng != "pe"
        waits = self._waits(eng, deps)
        tok = None
        key = None
        if signal:
            self.ecnt[eng] += 1
            key = ("e", eng)
            tok = (key, self.ecnt[eng])
        self.q[eng].append((waits, fn, key, 1))
        return tok

    def dma(self, eng, out, in_, deps=()):
        i = self.dnext[eng]
        self.dnext[eng] = (i + 1) % self.ndma
        key = ("d", eng, i)
        prev = self.dcnt[(eng, i)]
        deps = [deps]
        if prev > 0:
            deps.append((key, prev))
        waits = self._waits(eng, deps)
        self.dcnt[(eng, i)] = prev + 16
        self.q[eng].append((waits, (lambda e, o=out, s=in_: e.dma_start(out=o, in_=s)), key, 16))
        return (key, prev + 16)

    def finalize(self):
        for e in ("sp", "pool", "act"):
            deps = [(("d", e, i), self.dcnt[(e, i)]) for i in range(self.ndma) if self.dcnt[(e, i)] > 0]
            waits = self._waits(e, deps)
            if waits:
                self.q[e].append((waits, None, None, 0))

    def run(self, block):
        m = {"pe": block.tensor, "act": block.scalar, "dve": block.vector, "pool": block.gpsimd, "sp": block.sync}
        for eng in self.ENG:
            ops = self.q[eng]

            def body(e, ops=ops, eng=eng):
                if eng in self.need_pid:
                    self.pid[eng] = e.partition_id()
                for waits, fn, key, inc in ops:
                    for (k, v) in waits:
                        e.wait_ge(self.sems[k], v)
                    if fn is None:
                        continue
                    ins = fn(e)
                    if key is not None:
                        ins.then_inc(self.sems[key], inc)

            m[eng](body)


class Arena:
    NBYTES = 207 * 1024

    def __init__(self, nc, stack):
        self.t = stack.enter_context(nc.sbuf_tensor("arena", [128, self.NBYTES // 2], BF16))
        self.off = 0

    def reset(self):
        self.off = 0

    def alloc(self, shape, dt):
        esz = 4 if dt == F32 else 2
        n = 1
        for d_ in shape[1:]:
            n *= d_
        nbytes = (n * esz + 31) // 32 * 32
        assert self.off + nbytes <= self.NBYTES, f"arena overflow {self.off + nbytes}"
        ap = self.t[:, self.off // 2:(self.off + n * esz) // 2]
        self.off += nbytes
        if dt != BF16:
            ap = ap.bitcast(dt)
        if len(shape) == 3:
            ap = ap.rearrange("p (a b) -> p a b", a=shape[1])
        elif len(shape) == 4:
            ap = ap.rearrange("p (a b c) -> p a b c", a=shape[1], b=shape[2])
        if shape[0] != 128:
            ap = ap[0:shape[0]]
        return ap


class Users:
    def __init__(self, n=1):
        self.u = [[] for _ in range(n)]

    def take(self, i=0):
        r = self.u[i]
        self.u[i] = []
        return r

    def add(self, i, *toks):
        self.u[i].extend(t for t in toks if t is not None)


class Banks:
    def __init__(self, nc, stack):
        self.t = [stack.enter_context(nc.psum_tensor(f"ps{i}", [128, 512], F32)) for i in range(8)]
        self.last = [None] * 8

    def dep(self, i):
        return self.last[i]

    def set(self, i, tok):
        if tok is not None:
            self.last[i] = tok


def emit_rstd(S, ss_ap, n, out_ap, tmp_ap, eps_ap, deps):
    t = S.op("dve", lambda e: e.tensor_reduce(out=tmp_ap, in_=ss_ap, op=ALU.add, axis=AX.X), deps=deps)
    t = S.op("act", lambda e: e.activation(out=tmp_ap, in_=tmp_ap, func=AF.Ln, bias=eps_ap, scale=1.0 / n), deps=[t])
    t = S.op("act", lambda e: e.activation(out=out_ap, in_=tmp_ap, func=AF.Exp, scale=-0.5), deps=[t])
    return t


def emit_phase_b(nc, S, stack, PS, io, AR, cat_ready=(), fused=False):
    catT = io["catT"]
    xrows = io["xrows"]
    w_out, w_gate, w_up, w_down = io["w_out"], io["w_gate"], io["w_up"], io["w_down"]
    g_post, g2T_d, g_fpost = io["g_post"], io["g2T"], io["g_fpost"]
    out = io["out"]
    mixed_s, f_s, h1_s = io["mixed_s"], io["f_s"], io["h1_s"]
    wout_rt = io["wout_rt"]

    TT, TB, NTT = 512, 4, TOK_PER_CORE // 512
    PW = 512
    NPC = D // PW

    def sb(name, shape, dt):
        return AR.alloc(shape, dt)

    hffT = sb("hffT", [128, FT, TT], BF16)
    u2T = sb("u2T", [128, KT, TT], BF16)
    NW = 4
    wring = [sb(f"wr{i}", [128, 4096], BF16) for i in range(NW)]
    wr_u = Users(NW)
    wr_i = [0]
    rbH = sb("rbH", [128, D], F32)
    PWL = 2048
    NPCL = D // PWL
    _scr = hffT[:, 0:48, :].rearrange("p a b -> p (a b)").bitcast(F32)
    rbM = [_scr[:, (0 + i) * PWL:(1 + i) * PWL] for i in range(2)]
    rbX = [_scr[:, (2 + i) * PWL:(3 + i) * PWL] for i in range(2)]
    rbG = [_scr[:, (4 + i) * PWL:(5 + i) * PWL] for i in range(2)]
    scratch_users = []
    u2p = [sb(f"u2p{i}", [128, PW], BF16) for i in range(2)]
    mc = [sb(f"mc{i}", [128, 512], F32) for i in range(2)]
    sg = [sb(f"sg{i}", [128, 512], F32) for i in range(2)]
    fT_sb = [sb(f"fT{i}", [128, 4, TT], F32) for i in range(2)]
    junk = sb("junkb", [128, 2048], BF16)
    ss1 = sb("ss1", [128, TB, 8], F32)
    ss2 = sb("ss2", [128, TB, D // 2048], F32)
    ss3 = sb("ss3", [128, TB, 8], F32)
    rs = sb("rs", [128, 16], F32)
    tmpv = sb("tmpv", [128, 16], F32)
    epsb = sb("epsb", [128, 1], F32)
    g2T = sb("g2T_sb", [128, KT], F32)
    identb = sb("identb_sb", [128, 128], BF16)
    identf = sb("identf_sb", [128, 128], F32)
    rbM_u, rbX_u, rbG_u, rbO_u, u2p_u = Users(2), Users(2), Users(2), Users(2), Users(2)
    mc_u, sg_u, fT_u = Users(2), Users(2), Users(2)
    rbH_u = Users(1)
    junk_t = [None]

    t_eps = S.op("dve", lambda e: e.memset(epsb[:, :], RMS_EPS))
    t_g2 = S.dma("sp", g2T[:, :], g2T_d[:, :])
    t_ib = S.dma("sp", identb[:, :], io["identb"][:, :])
    t_if = S.dma("sp", identf[:, :], io["identf"][:, :])

    def wload(view_fn, src_ap):
        i = wr_i[0] % NW
        wr_i[0] += 1
        tok = S.dma("pool", view_fn(wring[i]), src_ap, deps=wr_u.take(i))
        return i, tok

    wv_out = w_out.rearrange("(kt p) c -> p kt c", p=128)
    wv_gate = w_gate.rearrange("(kt p) c -> p kt c", p=128)
    wv_up = w_up.rearrange("(kt p) c -> p kt c", p=128)
    wv_down = w_down.rearrange("(j p) c -> p j c", p=128)
    catv = catT.rearrange("(kt p) t -> p kt t", p=128) if catT is not None else None

    u2T_readers = []
    hff_readers = []
    h1_written = {}

    for tt in range(NTT):
        r0 = tt * TT
        cat_toks = [None] * KT
        if not fused:
            for q4 in range(4):
                tq4 = S.dma("sp", u2T[:, q4 * 8:(q4 + 1) * 8, :], catv[:, q4 * 8:(q4 + 1) * 8, r0:r0 + TT],
                            deps=[u2T_readers, cat_ready])
                for k_ in range(8):
                    cat_toks[q4 * 8 + k_] = tq4
        else:
            for half, agt in enumerate((io["ag_attn"], io["ag_rec"])):
                waits = S._waits("sp", [u2T_readers, cat_ready])
                i_ = S.dnext["sp"]
                S.dnext["sp"] = (i_ + 1) % S.ndma
                key = ("d", "sp", i_)
                prev = S.dcnt[("sp", i_)]
                waits += S._waits("sp", [(key, prev)] if prev > 0 else [])
                S.dcnt[("sp", i_)] = prev + 16
                S.need_pid.add("sp")
                S.q["sp"].append((waits, (lambda e, agt=agt, half=half, r0=r0: e.dma_start(
                    out=u2T[:, half * 16:(half + 1) * 16, :].rearrange("p (k o) w -> p k o w", o=1),
                    in_=agt.rearrange("(k j p) w -> p k j w", k=16, j=NCORES, p=128)[:, :, bass.ds(S.pid["sp"], 1), r0:r0 + TT])), key, 16))
                for k_ in range(16):
                    cat_toks[half * 16 + k_] = (key, prev + 16)
        u2T_readers = []
        mixed_w = {}
        for c in range(8):
            bset = [0, 1, 2, 3] if c % 2 == 0 else [4, 5, 6, 7]
            last = [None] * TB
            for pc in range(4):
                rt0 = pc * 8
                wi, wt = wload(lambda w: w[:, :].rearrange("p (k c) -> p k c", k=8),
                               wv_out[:, rt0:rt0 + 8, c * 512:(c + 1) * 512])
                wv = wring[wi][:, :].rearrange("p (k c) -> p k c", k=8)
                for tb in range(TB):
                    for k8 in range(8):
                        kt = pc * 8 + k8
                        first = (pc == 0 and k8 == 0)
                        lastmm = (pc == 3 and k8 == 7)
                        deps = [wt, cat_toks[kt]]
                        if first:
                            deps.append(PS.dep(bset[tb]))
                        sig = lastmm or (tb == TB - 1 and k8 == 7)
                        tk = S.op("pe", (lambda e, b=bset[tb], kt=kt, tb=tb, k8=k8, wv=wv, first=first, lastmm=lastmm:
                                         e.matmul(PS.t[b][:, :], lhsT=u2T[:, kt, tb * 128:(tb + 1) * 128], rhs=wv[:, k8, :],
                                                  start=first, stop=lastmm)),
                                  deps=deps, signal=sig)
                        if lastmm:
                            last[tb] = tk
                            PS.set(bset[tb], tk)
                wr_u.add(wi, tk)
            u2T_readers.append(last[TB - 1])
            for tb in range(TB):
                b = bset[tb]
                mi = (c * TB + tb) % 2
                t1 = S.op("act", (lambda e, b=b, tb=tb, c=c: e.activation(out=junk[:, 0:512], in_=PS.t[b][:, :], func=AF.Square,
                                                                    accum_out=ss1[:, tb, c:c + 1])),
                          deps=[PS.dep(b)])
                junk_t[0] = t1
                t2 = S.op("act", (lambda e, b=b, mi=mi: e.activation(out=mc[mi][:, :], in_=PS.t[b][:, :], func=AF.Copy)),
                          deps=[t1, mc_u.take(mi)])
                PS.set(b, t2)
                tw = S.dma("sp", mixed_s[r0 + tb * 128:r0 + (tb + 1) * 128, c * 512:(c + 1) * 512], mc[mi][:, :], deps=[t2])
                mc_u.add(mi, tw)
                mixed_w[(tb, c)] = tw

        u2_w = []
        for tb in range(TB):
            rows = slice(r0 + tb * 128, r0 + (tb + 1) * 128)
            t_r1 = emit_rstd(S, ss1[:, tb, :], D, rs[:, 0:1], tmpv[:, 0:1], epsb[:, 0:1], deps=[junk_t[0], t_eps])
            sq_toks = []
            hdeps = rbH_u.take(0)
            tlast = None
            for pc in range(NPCL):
                cs = slice(pc * PWL, (pc + 1) * PWL)
                i2 = pc % 2
                lm = S.dma("sp", rbM[i2][:, :], mixed_s[rows, cs], deps=[[mixed_w[(tb, c_)] for c_ in range(pc * 4, pc * 4 + 4)], rbM_u.take(i2), hff_readers])
                lx = S.dma("sp", rbX[i2][:, :], xrows[rows, cs], deps=[rbX_u.take(i2), hff_readers])
                lg = S.dma("sp", rbG[i2][:, :], g_post[0, cs].partition_broadcast(128), deps=[rbG_u.take(i2), hff_readers])
                ta = S.op("dve", (lambda e, i2=i2: e.scalar_tensor_tensor(out=rbM[i2][:, :], in0=rbM[i2][:, :], scalar=rs[:, 0:1],
                                                                           in1=rbG[i2][:, :], op0=ALU.mult, op1=ALU.mult)),
                          deps=[lm, lg, t_r1])
                rbG_u.add(i2, ta)
                tb_ = S.op("dve", (lambda e, i2=i2, cs=cs: e.tensor_tensor(out=rbH[:, cs], in0=rbM[i2][:, :], in1=rbX[i2][:, :], op=ALU.add)),
                           deps=[ta, lx, hdeps])
                rbM_u.add(i2, tb_)
                rbX_u.add(i2, tb_)
                tq = S.op("act", (lambda e, cs=cs, tb=tb, pc=pc: e.activation(out=junk[:, :], in_=rbH[:, cs], func=AF.Square,
                                                                             accum_out=ss2[:, tb, pc:pc + 1])),
                          deps=[tb_])
                junk_t[0] = tq
                sq_toks.append(tq)
                tlast = tb_
                scratch_users.append(tb_)
            th = S.dma("sp", h1_s[rows, :], rbH[:, :], deps=[tlast])
            h1_written[(tt, tb)] = th
            t_r2 = emit_rstd(S, ss2[:, tb, :], D, rs[:, 1:2], tmpv[:, 1:2], epsb[:, 0:1], deps=sq_toks)
            tu = None
            for pc in range(NPC):
                cs = slice(pc * PW, (pc + 1) * PW)
                i2 = pc % 2
                tu = S.op("act", (lambda e, i2=i2, cs=cs: e.activation(out=u2p[i2][:, :], in_=rbH[:, cs], func=AF.Copy, scale=rs[:, 1:2])),
                          deps=[t_r2, u2p_u.take(i2)])
                b = 4 + (pc % 2)
                pv = PS.t[b][:, 0:256].bitcast(BF16).rearrange("p (a t) -> p a t", a=4)
                tp = None
                for j in range(4):
                    tp = S.op("pe", (lambda e, i2=i2, j=j, pv=pv: e.transpose(out=pv[:, j, :], in_=u2p[i2][:, j * 128:(j + 1) * 128],
                                                                            identity=identb[:, :])),
                              deps=[tu, t_ib, PS.dep(b) if j == 0 else None], signal=(j == 3))
                u2p_u.add(i2, tp)
                kt0 = pc * 4
                te = S.op("dve", (lambda e, pv=pv, kt0=kt0, tb=tb: e.tensor_tensor(
                    out=u2T[:, kt0:kt0 + 4, tb * 128:(tb + 1) * 128], in0=pv,
                    in1=g2T[:, kt0:kt0 + 4].unsqueeze(2).to_broadcast([128, 4, 128]), op=ALU.mult)),
                          deps=[tp, t_g2, u2T_readers if (pc == 0 and tb == 0) else None])
                PS.set(b, te)
                u2_w.append(te)
            rbH_u.add(0, th, tu)
        u2T_readers = []

        hdeps_once = hff_readers
        hff_readers = []
        hff_w = {}
        for j2 in range(FT // 2):
            bset = [0, 1, 2, 3] if j2 % 2 == 0 else [4, 5, 6, 7]
            last = [None] * 4
            for mi_, wv_src in ((0, wv_gate), (1, wv_up)):
                for half in range(2):
                    wi, wt = wload(lambda w: w[:, :].rearrange("p (k c) -> p k c", k=16),
                                   wv_src[:, half * 16:(half + 1) * 16, j2 * 256:(j2 + 1) * 256])
                    wv = wring[wi][:, :].rearrange("p (k c) -> p k c", k=16)
                    for f in range(2):
                        bi = mi_ * 2 + f
                        for k in range(16):
                            kt = half * 16 + k
                            first = (half == 0 and k == 0)
                            lastmm = (half == 1 and k == 15)
                            deps = [wt]
                            if first:
                                deps.append(PS.dep(bset[bi]))
                                if j2 == 0 and mi_ == 0 and f == 0:
                                    deps.append(u2_w)
                            sig = lastmm or (f == 1 and k == 15)
                            tk = S.op("pe", (lambda e, b=bset[bi], kt=kt, k=k, f=f, wv=wv, first=first, lastmm=lastmm:
                                             e.matmul(PS.t[b][:, :], lhsT=wv[:, k, f * 128:(f + 1) * 128], rhs=u2T[:, kt, :],
                                                      start=first, stop=lastmm)),
                                      deps=deps, signal=sig)
                            if lastmm:
                                last[bi] = tk
                                PS.set(bset[bi], tk)
                    wr_u.add(wi, tk)
            u2T_readers.append(last[3])
            for f in range(2):
                j = j2 * 2 + f
                si = j % 2
                t1 = S.op("act", (lambda e, b=bset[f], si=si: e.activation(out=sg[si][:, :], in_=PS.t[b][:, :], func=AF.Silu)),
                          deps=[last[f], sg_u.take(si)])
                PS.set(bset[f], t1)
                t2 = S.op("dve", (lambda e, b=bset[2 + f], si=si, j=j: e.tensor_tensor(out=hffT[:, j, :], in0=PS.t[b][:, :], in1=sg[si][:, :],
                                                                                     op=ALU.mult)),
                          deps=[t1, last[2 + f], [hdeps_once, scratch_users] if j == 0 else None])
                sg_u.add(si, t2)
                PS.set(bset[2 + f], t2)
                hff_w[j] = t2

        f_w = {}
        pend = None
        NJP = (FT + 7) // 8

        def emit_transposes(g, fi, t_ev):
            for tb in range(TB):
                b = 4 + tb
                tp = None
                for dt_ in range(4):
                    tp = S.op("pe", (lambda e, b=b, dt_=dt_, fi=fi, tb=tb: e.transpose(
                        out=PS.t[b][:, dt_ * 128:(dt_ + 1) * 128], in_=fT_sb[fi][:, dt_, tb * 128:(tb + 1) * 128], identity=identf[:, :])),
                              deps=[t_ev[dt_], t_if, PS.dep(b) if dt_ == 0 else None], signal=(dt_ == 3))
                fT_u.add(fi, tp)
                t1 = S.op("act", (lambda e, b=b, tb=tb, g=g: e.activation(out=junk[:, 0:512], in_=PS.t[b][:, :], func=AF.Square,
                                                                         accum_out=ss3[:, tb, g:g + 1])),
                          deps=[tp])
                junk_t[0] = t1
                mi = (g * TB + tb) % 2
                t2 = S.op("act", (lambda e, b=b, mi=mi: e.activation(out=mc[mi][:, :], in_=PS.t[b][:, :], func=AF.Copy)),
                          deps=[t1, mc_u.take(mi)])
                PS.set(b, t2)
                tw = S.dma("sp", f_s[r0 + tb * 128:r0 + (tb + 1) * 128, g * 512:(g + 1) * 512], mc[mi][:, :], deps=[t2])
                mc_u.add(mi, tw)
                f_w[(tb, g)] = tw

        for g in range(8):
            bset = [0, 1, 2, 3]
            last = [None] * 4
            for jp in range(NJP):
                j0 = jp * 8
                nj = min(8, FT - j0)
                wi, wt = wload(lambda w, nj=nj: w[:, 0:nj * 512].rearrange("p (k c) -> p k c", k=nj),
                               wv_down[:, j0:j0 + nj, g * 512:(g + 1) * 512])
                wv = wring[wi][:, 0:nj * 512].rearrange("p (k c) -> p k c", k=nj)
                for jj in range(nj):
                    j = j0 + jj
                    for dt_ in range(4):
                        first = (j == 0)
                        lastmm = (j == FT - 1)
                        deps = [wt, hff_w[j]]
                        if first:
                            deps.append(PS.dep(bset[dt_]))
                        sig = lastmm or (jj == nj - 1 and dt_ == 3)
                        tk = S.op("pe", (lambda e, b=bset[dt_], jj=jj, j=j, dt_=dt_, wv=wv, first=first, lastmm=lastmm:
                                         e.matmul(PS.t[b][:, :], lhsT=wv[:, jj, dt_ * 128:(dt_ + 1) * 128], rhs=hffT[:, j, :],
                                                  start=first, stop=lastmm)),
                                  deps=deps, signal=sig)
                        if lastmm:
                            last[dt_] = tk
                            PS.set(bset[dt_], tk)
                wr_u.add(wi, tk)
                if jp == 1 and pend is not None:
                    emit_transposes(*pend)
                    pend = None
            hff_readers.append(last[3])
            fi = g % 2
            t_ev = []
            fdeps = fT_u.take(fi)
            for dt_ in range(4):
                t1 = S.op("act", (lambda e, b=bset[dt_], fi=fi, dt_=dt_: e.activation(out=fT_sb[fi][:, dt_, :], in_=PS.t[b][:, :], func=AF.Copy)),
                          deps=[last[dt_], fdeps])
                PS.set(bset[dt_], t1)
                t_ev.append(t1)
            pend = (g, fi, t_ev)
        emit_transposes(*pend)

        for tb in range(TB):
            rows = slice(r0 + tb * 128, r0 + (tb + 1) * 128)
            t_r3 = emit_rstd(S, ss3[:, tb, :], D, rs[:, 2:3], tmpv[:, 2:3], epsb[:, 0:1], deps=[junk_t[0], t_eps])
            for pc in range(NPCL):
                cs = slice(pc * PWL, (pc + 1) * PWL)
                i2 = pc % 2
                lm = S.dma("sp", rbM[i2][:, :], f_s[rows, cs], deps=[[f_w[(tb, c_)] for c_ in range(pc * 4, pc * 4 + 4)], rbM_u.take(i2), hff_readers])
                lx = S.dma("sp", rbX[i2][:, :], h1_s[rows, cs], deps=[rbX_u.take(i2), h1_written[(tt, tb)], hff_readers])
                lg = S.dma("sp", rbG[i2][:, :], g_fpost[0, cs].partition_broadcast(128), deps=[rbG_u.take(i2), hff_readers])
                ta = S.op("dve", (lambda e, i2=i2: e.scalar_tensor_tensor(out=rbM[i2][:, :], in0=rbM[i2][:, :], scalar=rs[:, 2:3],
                                                                           in1=rbG[i2][:, :], op0=ALU.mult, op1=ALU.mult)),
                          deps=[lm, lg, t_r3])
                rbG_u.add(i2, ta)
                to = S.op("dve", (lambda e, i2=i2: e.tensor_tensor(out=rbM[i2][:, :], in0=rbM[i2][:, :], in1=rbX[i2][:, :], op=ALU.add)),
                          deps=[ta, lx])
                rbX_u.add(i2, to)
                tw = S.dma("sp", out[rows, cs], rbM[i2][:, :], deps=[to])
                rbM_u.add(i2, tw)
                scratch_users.append(tw)
                scratch_users.append(to)


def build_b():
    nc = bass.Bass("TRN2", target_bir_lowering=False)
    io = {}
    io["catT"] = nc.dram_tensor("catT", [D, TOK_PER_CORE], BF16, kind="ExternalInput").ap()
    io["xrows"] = nc.dram_tensor("xrows", [TOK_PER_CORE, D], F32, kind="ExternalInput").ap()
    io["w_out"] = nc.dram_tensor("w_out", [D, D], F32, kind="ExternalInput").ap()
    io["w_gate"] = nc.dram_tensor("w_gate", [D, FF], F32, kind="ExternalInput").ap()
    io["w_up"] = nc.dram_tensor("w_up", [D, FF], F32, kind="ExternalInput").ap()
    io["w_down"] = nc.dram_tensor("w_down", [FF, D], F32, kind="ExternalInput").ap()
    io["g_post"] = nc.dram_tensor("g_post", [1, D], F32, kind="ExternalInput").ap()
    io["g_fpost"] = nc.dram_tensor("g_fpost", [1, D], F32, kind="ExternalInput").ap()
    io["g2T"] = nc.dram_tensor("g2T", [128, KT], F32, kind="ExternalInput").ap()
    io["identb"] = nc.dram_tensor("identb", [128, 128], BF16, kind="ExternalInput").ap()
    io["identf"] = nc.dram_tensor("identf", [128, 128], F32, kind="ExternalInput").ap()
    io["out"] = nc.dram_tensor("out", [TOK_PER_CORE, D], F32, kind="ExternalOutput").ap()
    io["mixed_s"] = nc.dram_tensor("mixed_s", [TOK_PER_CORE, D], F32, kind="Internal").ap()
    io["f_s"] = nc.dram_tensor("f_s", [TOK_PER_CORE, D], F32, kind="Internal").ap()
    io["h1_s"] = nc.dram_tensor("h1_s", [TOK_PER_CORE, D], F32, kind="Internal").ap()
    io["wout_rt"] = cat_rowtile_map()
    with ExitStack() as stack:
        S = Sched(nc, stack)
        PS = Banks(nc, stack)
        AR = Arena(nc, stack)
        emit_phase_b(nc, S, stack, PS, io, AR)
        S.finalize()
        with nc.Block() as block:
            S.run(block)
    return nc


def cat_rowtile_map():
    m = []
    for r in range(NCORES):
        for j in range(4):
            if j < 2:
                m.append(r * 2 + j)
            else:
                m.append(16 + r * 2 + (j - 2))
    return m


def host_consts():
    identb = np.eye(128, dtype=np.float32).astype(ml_dtypes.bfloat16)
    identf = np.eye(128, dtype=np.float32)
    return identb, identf


def run_phase_b(catT_all, inputs, cores=None):
    nc = build_b()
    identb, identf = host_consts()
    x = np.ascontiguousarray(inputs["x"]).reshape(NBATCH * S_LEN, D)
    g2T = np.ascontiguousarray(np.asarray(inputs["ffn_pre_g"][0]).reshape(KT, 128).T)
    in_maps = []
    cores = list(range(NCORES)) if cores is None else cores
    for c in cores:
        sl = slice(c * TOK_PER_CORE, (c + 1) * TOK_PER_CORE)
        in_maps.append({
            "catT": np.ascontiguousarray(catT_all[:, sl]),
            "xrows": np.ascontiguousarray(x[sl]),
            "w_out": np.asarray(inputs["w_out"][0]),
            "w_gate": np.asarray(inputs["w_gate"][0]),
            "w_up": np.asarray(inputs["w_up"][0]),
            "w_down": np.asarray(inputs["w_down"][0]),
            "g_post": np.asarray(inputs["mix_post_g"]).reshape(1, D),
            "g_fpost": np.asarray(inputs["ffn_post_g"]).reshape(1, D),
            "g2T": g2T,
            "identb": identb,
            "identf": identf,
        })
    res = run_bass_kernel_spmd(nc, in_maps, core_ids=list(range(len(cores))))
    outs = [np.asarray(r["out"]) for r in res.results]
    return np.concatenate(outs, axis=0)


CH = 256
GELU_K = 2.0 * math.sqrt(2.0 / math.pi)


def emit_phase_a(nc, S, stack, PS, io, AR, fused=False):
    x_all, meta = io["x_all"], io["meta"]
    w_in = io["w_in_c"]
    cosT, sinT = io["cosT"], io["sinT"]
    cat_out = io["cat_out"]

    def sb(name, shape, dt):
        return AR.alloc(shape, dt)

    NTL = CH // 128
    Wb = sb("Wb", [128, KT, 1280], BF16)
    xa = sb("xa", [128, D], F32)
    xn = sb("xn", [128, D], BF16)
    uT = sb("uT", [128, KT, CH], BF16)
    KTs = sb("KTs", [128, 2, TPOS], BF16)
    Vs = sb("Vs", [128, 33, 257], BF16)
    QTs = sb("QTs", [128, 2, CH], BF16)
    cs_c = sb("cs_c", [128, CH], F32)
    sn_c = sb("sn_c", [128, CH], F32)
    ropeA = sb("ropeA", [128, CH], F32)
    ropeB = sb("ropeB", [128, CH], F32)
    XR = [sb(f"XR{i}", [128, 3 + CH], F32) for i in range(2)]
    GT = [sb(f"GT{i}", [128, CH], F32) for i in range(2)]
    LT = [{k: sb(f"lt_{k}{i}", [128, CH], F32) for k in ("xc", "t1", "t2", "r", "a", "om", "bt", "H", "g1", "g2")} for i in range(2)]
    xcb = [sb(f"xcb{i}", [128, CH], BF16) for i in range(2)]
    Eb = [sb(f"Eb{i}", [128, 2, 2, 128], BF16) for i in range(2)]
    Ob = sb("Ob", [128, 256], F32)
    ATb = sb("ATb", [128, 256], F32)
    Yb = sb("Yb", [128, 256], BF16)
    catS = [sb(f"catS{i}", [128, 4, CH], BF16) for i in range(2)]
    st = sb("stat", [128, 16], F32)
    state = sb("state", [128, 2], F32)
    state_m = sb("state_m", [128, 2], F32)
    hist_m = sb("hist_m", [128, 2, 3], F32)
    LC = sb("LC", [128, 2, 12], F32)
    lamb = sb("lamb", [128, 4, 128], F32)
    lamt = sb("lamt", [128, 8], F32)
    gsub = sb("gsub", [128, 256], F32)
    epsb = sb("epsA", [128, 1], F32)
    g1T = sb("g1T_sb", [128, KT], F32)
    identb = sb("identA", [128, 128], BF16)
    maskb = sb("maskb", [128, 128], BF16)
    WR = sb("WR", [128, 2, 128], BF16)
    WI = sb("WI", [128, 2, 128], BF16)

    wv = w_in.rearrange("(kt p) c -> p kt c", p=128)
    t_w = [S.dma("pool", Wb[:, q * 8:(q + 1) * 8, :], wv[:, q * 8:(q + 1) * 8, :]) for q in range(4)]
    t_wr = S.dma("pool", WR[:, :, :], io["w_r_c"].rearrange("h i j -> i h j"))
    t_wi = S.dma("pool", WI[:, :, :], io["w_i_c"].rearrange("h i j -> i h j"))
    t_g1 = S.dma("sp", g1T[:, :], io["g1T"][:, :])
    t_ib = S.dma("sp", identb[:, :], io["identb"][:, :])
    t_mk = S.dma("sp", maskb[:, :], io["mask"][:, :])
    t_lp = S.dma("sp", LC[:, :, 0:8], io["lru_p"][:, :, :])
    t_lv = S.dma("sp", lamb[:, :, :].rearrange("p a b -> p (a b)"), io["lamv"].rearrange("a b -> (a b)").partition_broadcast(128))
    t_sg = S.dma("sp", gsub[:, :], io["subg"][0, :].partition_broadcast(128))
    t_eps = S.op("dve", lambda e: e.memset(epsb[:, :], RMS_EPS))
    t_ones = S.op("dve", lambda e: e.memset(Vs[:, :, 256:257], 1.0))
    t_z1 = S.op("dve", lambda e: e.memset(state[:, :], 0.0))
    t_z2 = [S.op("dve", lambda e, i=i: e.memset(XR[i][:, 0:3], 0.0)) for i in range(2)]
    t = S.op("dve", lambda e: e.tensor_tensor(out=lamb[:, 0, :], in0=lamb[:, 0, :], in1=lamb[:, 1, :], op=ALU.mult), deps=[t_lv])
    t = S.op("dve", lambda e: e.tensor_tensor(out=lamb[:, 2, :], in0=lamb[:, 2, :], in1=lamb[:, 3, :], op=ALU.mult), deps=[t])
    t = S.op("dve", lambda e: e.tensor_reduce(out=lamt[:, 0:1], in_=lamb[:, 0, :], op=ALU.add, axis=AX.X), deps=[t])
    t = S.op("dve", lambda e: e.tensor_reduce(out=lamt[:, 1:2], in_=lamb[:, 2, :], op=ALU.add, axis=AX.X), deps=[t])
    t = S.op("act", lambda e: e.activation(out=lamt[:, 2:4], in_=lamt[:, 0:2], func=AF.Exp), deps=[t])
    t = S.op("dve", lambda e: e.tensor_tensor(out=lamt[:, 4:5], in0=lamt[:, 3:4], in1=lamt[:, 2:3], op=ALU.subtract), deps=[t])
    t_nlam = S.op("dve", lambda e: e.tensor_scalar(out=lamt[:, 4:5], in0=lamt[:, 4:5], scalar1=-LAMBDA_INIT, scalar2=None, op0=ALU.add), deps=[t])
    t_gs = S.op("dve", lambda e: e.tensor_scalar(out=gsub[:, :], in0=gsub[:, :], scalar1=1.0 - LAMBDA_INIT, scalar2=None, op0=ALU.mult), deps=[t_sg])
    t = S.op("act", lambda e: e.activation(out=LC[:, :, 10:11], in_=LC[:, :, 7:8], func=AF.Exp, scale=-1.0), deps=[t_lp])
    t = S.op("act", lambda e: e.activation(out=LC[:, :, 10:11], in_=LC[:, :, 10:11], func=AF.Ln, bias=1.0), deps=[t])
    t_c1 = S.op("dve", lambda e: e.tensor_scalar(out=LC[:, :, 7:8], in0=LC[:, :, 10:11], scalar1=-8.0, scalar2=None, op0=ALU.mult), deps=[t])
    t_c2 = S.op("dve", lambda e: e.tensor_scalar(out=LC[:, :, 8:9], in0=LC[:, :, 10:11], scalar1=-16.0, scalar2=None, op0=ALU.mult), deps=[t_c1])
    t_nb = S.op("dve", lambda e: e.tensor_scalar(out=LC[:, :, 5:7], in0=LC[:, :, 5:7], scalar1=-1.0, scalar2=None, op0=ALU.mult), deps=[t_lp])
    t_lc = [t_c1, t_c2, t_nb]

    xa_u, xn_u = Users(1), Users(1)
    XR_u, GT_u = Users(2), Users(2)
    cs_u = Users(1)
    rope_u = Users(1)
    E_u = Users(2)
    catS_u = Users(2)
    Y_u = Users(1)
    state_t = [[t_z1], [t_z1]]
    hist_t = [[t_z2[0]], [t_z2[1]]]
    proj_rot = [0]
    misc_rot = [0]
    pend_store = [None]
    chunk_no = [0]

    def proj_bank():
        b = (0, 1)[proj_rot[0] % 2]
        proj_rot[0] += 1
        return b

    def misc_bank():
        b = (2, 7)[misc_rot[0] % 2]
        misc_rot[0] += 1
        return b

    def cat_store(pcur, pcol, pdeps):
        if not fused:
            return [S.dma("sp", cat_out.rearrange("(t p) n -> p t n", p=128)[:, :, pcol:pcol + CH], catS[pcur][:, :, :], deps=pdeps)]
        j_ = pcol // TOK_PER_CORE
        w0 = pcol % TOK_PER_CORE
        toks = []
        for half, dst in enumerate((io["snd_attn"], io["snd_rec"])):
            dv = dst.rearrange("(t j p) w -> p t j w", t=2, j=NCORES, p=128)
            toks.append(S.dma("sp", dv[:, :, j_, w0:w0 + CH], catS[pcur][:, half * 2:half * 2 + 2, :], deps=pdeps))
        return toks

    COL = {"q0": 0, "q1": 128, "k0": 256, "k1": 384, "v": 512, "xr0": 768, "xr1": 896, "g0": 1024, "g1": 1152}

    def step1(kind, b, ci):
        is_x = kind == "x"
        tiles = [(tl, 128) for tl in range(NTL)] if is_x else [(0, NMETA)]
        uT_w = []
        for tl, np_ in tiles:
            if is_x:
                r0 = b * S_LEN + ci * CH + tl * 128
                src = x_all[r0:r0 + np_, :]
            else:
                src = meta[0:np_, :]
            t_x = S.dma("sp", xa[0:np_, :], src, deps=xa_u.take(0))
            t_sq = S.op("act", lambda e, np_=np_: e.activation(out=xn[0:np_, :], in_=xa[0:np_, :], func=AF.Square, accum_out=st[0:np_, 0:1]),
                        deps=[t_x, xn_u.take(0)])
            t1 = S.op("act", lambda e, np_=np_: e.activation(out=st[0:np_, 1:2], in_=st[0:np_, 0:1], func=AF.Ln, bias=epsb[0:np_, 0:1], scale=1.0 / D),
                      deps=[t_sq, t_eps])
            t2 = S.op("act", lambda e, np_=np_: e.activation(out=st[0:np_, 2:3], in_=st[0:np_, 1:2], func=AF.Exp, scale=-0.5), deps=[t1])
            t_xn = S.op("act", lambda e, np_=np_: e.activation(out=xn[0:np_, :], in_=xa[0:np_, :], func=AF.Copy, scale=st[0:np_, 2:3]), deps=[t2])
            xa_u.add(0, t_xn)
            for q in range(4):
                bk = misc_bank()
                pv = PS.t[bk][:, :].bitcast(BF16).rearrange("p (a t) -> p a t", a=8)
                tp = None
                for j in range(8):
                    kt = q * 8 + j
                    tp = S.op("pe", lambda e, np_=np_, kt=kt, j=j, pv=pv: e.transpose(out=pv[:, j, 0:np_], in_=xn[0:np_, kt * 128:(kt + 1) * 128],
                                                                                 identity=identb[0:np_, 0:np_]),
                              deps=[t_xn, t_ib, PS.dep(bk) if j == 0 else None], signal=(j == 7))
                te = S.op("dve", lambda e, np_=np_, q=q, pv=pv, tl=tl: e.tensor_tensor(
                    out=uT[:, q * 8:(q + 1) * 8, tl * 128:tl * 128 + np_], in0=pv[:, :, 0:np_],
                    in1=g1T[:, q * 8:(q + 1) * 8].unsqueeze(2).to_broadcast([128, 8, np_]), op=ALU.mult),
                          deps=[tp, t_g1])
                PS.set(bk, te)
                uT_w.append(te)
            xn_u.add(0, tp)
        return uT_w

    def process_chunk(kind, b, ci, uT_w, nxt):
        is_x = kind == "x"
        n = CH if is_x else NMETA
        pos0 = NMETA + ci * CH if is_x else 0
        tiles = [(tl, 128) for tl in range(NTL)] if is_x else [(0, NMETA)]
        cur = chunk_no[0] % 2
        chunk_no[0] += 1
        cdeps = cs_u.take(0)
        t_cs = S.dma("sp", cs_c[:, 0:n], cosT[:, pos0:pos0 + n], deps=cdeps)
        t_sn = S.dma("sp", sn_c[:, 0:n], sinT[:, pos0:pos0 + n], deps=cdeps)
        names = ["k0", "k1"] + (["q0", "q1"] if is_x else []) + ["xr0", "xr1"] + (["g0", "g1"] if is_x else [])
        t_q = [None, None]
        t_k = [None, None]
        t_xr = [None, None]
        t_g = [None, None]
        for nm in names:
            bk = proj_bank()
            col = COL[nm]
            tk = None
            for kt in range(KT):
                tk = S.op("pe", lambda e, bk=bk, kt=kt, col=col: e.matmul(PS.t[bk][:, 0:n], lhsT=Wb[:, kt, col:col + 128], rhs=uT[:, kt, 0:n],
                                                                        start=(kt == 0), stop=(kt == KT - 1)),
                          deps=[uT_w, t_w, PS.dep(bk)] if kt == 0 else [], signal=(kt == KT - 1))
            PS.set(bk, tk)
            P = PS.t[bk]
            if nm[0] in "qk":
                comp = int(nm[1])
                dst = (QTs[:, comp, 0:n] if nm[0] == "q" else KTs[:, comp, pos0:pos0 + n])
                rd = rope_u.take(0)
                ta = S.op("dve", lambda e, P=P: e.tensor_tensor(out=ropeA[:, 0:n], in0=P[:, 0:n], in1=cs_c[:, 0:n], op=ALU.mult),
                          deps=[tk, t_cs, rd])
                tb1 = S.op("dve", lambda e, P=P: e.tensor_tensor(out=ropeB[0:64, 0:n], in0=P[64:128, 0:n], in1=sn_c[64:128, 0:n], op=ALU.mult),
                           deps=[ta, t_sn])
                tb_ = S.op("dve", lambda e, P=P: e.tensor_tensor(out=ropeB[64:128, 0:n], in0=P[0:64, 0:n], in1=sn_c[0:64, 0:n], op=ALU.mult),
                           deps=[tb1])
                PS.set(bk, tb_)
                th_ = S.op("dve", lambda e, dst=dst: e.tensor_tensor(out=dst, in0=ropeA[:, 0:n], in1=ropeB[:, 0:n], op=ALU.add),
                           deps=[tb_])
                rope_u.add(0, th_)
                cs_u.add(0, th_)
                if nm[0] == "q":
                    t_q[comp] = th_
                else:
                    t_k[comp] = th_
            elif nm[0] == "x":
                ct = int(nm[2])
                te = S.op("act", lambda e, P=P, ct=ct: e.activation(out=XR[ct][:, 3:3 + n], in_=P[:, 0:n], func=AF.Copy),
                          deps=[tk, XR_u.take(ct)])
                PS.set(bk, te)
                t_xr[ct] = te
            else:
                ct = int(nm[1])
                te = S.op("act", lambda e, P=P, ct=ct: e.activation(out=GT[ct][:, 0:n], in_=P[:, 0:n], func=AF.Copy),
                          deps=[tk, GT_u.take(ct)])
                PS.set(bk, te)
                t_g[ct] = te
        t_v = {}
        for tl, np_ in tiles:
            bk = misc_bank()
            vblk = (1 + ci * NTL + tl) if is_x else 0
            tk = None
            for kt in range(KT):
                tk = S.op("pe", lambda e, bk=bk, kt=kt, tl=tl, np_=np_: e.matmul(PS.t[bk][0:np_, 0:256], lhsT=uT[:, kt, tl * 128:tl * 128 + np_],
                                                                               rhs=Wb[:, kt, 512:768], start=(kt == 0), stop=(kt == KT - 1)),
                          deps=[uT_w, t_w, PS.dep(bk)] if kt == 0 else [], signal=(kt == KT - 1))
            te = S.op("act", lambda e, bk=bk, np_=np_, vblk=vblk: e.activation(out=Vs[0:np_, vblk, 0:256], in_=PS.t[bk][0:np_, 0:256], func=AF.Copy),
                      deps=[tk, t_ones])
            PS.set(bk, te)
            t_v[vblk] = te
            vready[vblk] = te
        if not is_x:
            kready["meta"] = t_k
        else:
            for tl in range(NTL):
                kready[ci * NTL + tl] = t_k
        uT_next = step1(*nxt) if nxt is not None else None
        pending = []

        def pump(k):
            for _ in range(k):
                for g_ in list(pending):
                    try:
                        next(g_)
                    except StopIteration:
                        pending.remove(g_)

        cat_wdeps = catS_u.take(cur)
        prev_norm = [None]
        def lru_gen(ct):
            L = LT[ct]
            yield
            cw = lambda k, ct=ct: LC[:, ct, k:k + 1]
            yield
            xr = XR[ct]
            yield
            xc = L["xc"]
            yield
            t = S.op("dve", lambda e, xr=xr, xc=xc, ct=ct: e.tensor_scalar(out=xc[:, 0:n], in0=xr[:, 3:3 + n], scalar1=LC[:, ct, 3:4], scalar2=LC[:, ct, 4:5],
                                                                      op0=ALU.mult, op1=ALU.add),
                     deps=[t_xr[ct], hist_t[ct], t_lp])
            yield
            for w in (2, 1, 0):
                t = S.op("dve", lambda e, xr=xr, xc=xc, ct=ct, w=w: e.scalar_tensor_tensor(out=xc[:, 0:n], in0=xr[:, w:w + n], scalar=LC[:, ct, w:w + 1],
                                                                                       in1=xc[:, 0:n], op0=ALU.mult, op1=ALU.add), deps=[t])
                yield
            t_xc = t
            yield
            th = S.op("pool", lambda e, xr=xr: e.tensor_copy(out=xr[:, 0:3], in_=xr[:, n:n + 3]), deps=[t_xc])
            yield
            hist_t[ct] = [th]
            yield
            XR_u.add(ct, th)
            yield
            tb16 = S.op("act", lambda e, xc=xc, ct=ct: e.activation(out=xcb[ct][:, 0:n], in_=xc[:, 0:n], func=AF.Copy), deps=[t_xc])
            yield
            bk = misc_bank()
            P = PS.t[bk]
            t_mr = S.op("pe", lambda e, P=P, ct=ct: e.matmul(P[:, 0:n], lhsT=WR[:, ct, :], rhs=xcb[ct][:, 0:n], start=True, stop=True),
                        deps=[tb16, t_wr, PS.dep(bk)], signal=False)
            t_mi = S.op("pe", lambda e, P=P, ct=ct: e.matmul(P[:, 256:256 + n], lhsT=WI[:, ct, :], rhs=xcb[ct][:, 0:n], start=True, stop=True),
                        deps=[t_wi], signal=True)
            PS.set(bk, t_mi)
            t = S.op("act", lambda e, P=P, L=L, ct=ct: e.activation(out=L["t1"][:, 0:n], in_=P[:, 0:n], func=AF.Exp, scale=-1.0, bias=LC[:, ct, 5:6]),
                     deps=[t_mi, t_lc])
            t = S.op("act", lambda e, L=L: e.activation(out=L["t1"][:, 0:n], in_=L["t1"][:, 0:n], func=AF.Ln, bias=1.0), deps=[t])
            t_r = S.op("act", lambda e, L=L: e.activation(out=L["r"][:, 0:n], in_=L["t1"][:, 0:n], func=AF.Exp, scale=-1.0), deps=[t])
            t = S.op("act", lambda e, P=P, L=L, ct=ct: e.activation(out=L["t2"][:, 0:n], in_=P[:, 256:256 + n], func=AF.Exp, scale=-1.0, bias=LC[:, ct, 6:7]),
                     deps=[t_mi, t_lc])
            PS.set(bk, t)
            yield
            yield
            t_spi = S.op("act", lambda e, L=L: e.activation(out=L["t2"][:, 0:n], in_=L["t2"][:, 0:n], func=AF.Ln, bias=1.0), deps=[t])
            yield
            t_a = S.op("act", lambda e, L=L, ct=ct: e.activation(out=L["a"][:, 0:n], in_=L["r"][:, 0:n], func=AF.Exp, scale=LC[:, ct, 7:8]), deps=[t_r, t_lc])
            yield
            t = S.op("act", lambda e, L=L, ct=ct: e.activation(out=L["om"][:, 0:n], in_=L["r"][:, 0:n], func=AF.Exp, scale=LC[:, ct, 8:9]), deps=[t_r, t_lc])
            yield
            t = S.op("pool", lambda e, L=L: e.tensor_scalar(out=L["om"][:, 0:n], in0=L["om"][:, 0:n], scalar1=-1.0, scalar2=1.0, op0=ALU.mult, op1=ALU.add), deps=[t])
            yield
            t = S.op("act", lambda e, L=L: e.activation(out=L["om"][:, 0:n], in_=L["om"][:, 0:n], func=AF.Ln), deps=[t])
            yield
            t = S.op("dve", lambda e, L=L: e.scalar_tensor_tensor(out=L["om"][:, 0:n], in0=L["om"][:, 0:n], scalar=0.5, in1=L["t2"][:, 0:n],
                                                                   op0=ALU.mult, op1=ALU.subtract), deps=[t, t_spi])
            yield
            t = S.op("act", lambda e, L=L: e.activation(out=L["om"][:, 0:n], in_=L["om"][:, 0:n], func=AF.Exp), deps=[t])
            yield
            t_bt = S.op("pool", lambda e, L=L, xc=xc: e.tensor_tensor(out=L["bt"][:, 0:n], in0=L["om"][:, 0:n], in1=xc[:, 0:n], op=ALU.mult), deps=[t, t_xc])
            yield
            t_H = S.op("dve", lambda e, L=L, ct=ct: e.tensor_tensor_scan(out=L["H"][:, 0:n], data0=L["a"][:, 0:n], data1=L["bt"][:, 0:n],
                                                                      initial=state[:, ct:ct + 1], op0=ALU.mult, op1=ALU.add),
                       deps=[t_a, t_bt, state_t[ct]])
            yield
            t_st = S.op("act", lambda e, L=L, ct=ct: e.activation(out=state[:, ct:ct + 1], in_=L["H"][:, n - 1:n], func=AF.Copy), deps=[t_H])
            yield
            state_t[ct] = [t_st]
            yield
            if is_x:
                gt = GT[ct]
                yield
                t = S.op("pool", lambda e, L=L, gt=gt: e.tensor_tensor(out=L["g1"][:, 0:n], in0=gt[:, 0:n], in1=gt[:, 0:n], op=ALU.mult), deps=[t_g[ct]])
                yield
                t = S.op("pool", lambda e, L=L: e.tensor_scalar(out=L["g1"][:, 0:n], in0=L["g1"][:, 0:n], scalar1=0.044715, scalar2=1.0,
                                                                op0=ALU.mult, op1=ALU.add), deps=[t])
                yield
                t = S.op("pool", lambda e, L=L, gt=gt: e.tensor_tensor(out=L["g1"][:, 0:n], in0=L["g1"][:, 0:n], in1=gt[:, 0:n], op=ALU.mult), deps=[t])
                yield
                t = S.op("act", lambda e, L=L: e.activation(out=L["g2"][:, 0:n], in_=L["g1"][:, 0:n], func=AF.Exp, scale=-GELU_K), deps=[t])
                yield
                t = S.op("act", lambda e, L=L: e.activation(out=L["g2"][:, 0:n], in_=L["g2"][:, 0:n], func=AF.Ln, bias=1.0), deps=[t])
                yield
                t = S.op("act", lambda e, L=L: e.activation(out=L["g2"][:, 0:n], in_=L["g2"][:, 0:n], func=AF.Exp, scale=-1.0), deps=[t])
                yield
                t = S.op("pool", lambda e, L=L, gt=gt: e.tensor_tensor(out=L["g2"][:, 0:n], in0=L["g2"][:, 0:n], in1=gt[:, 0:n], op=ALU.mult), deps=[t])
                yield
                GT_u.add(ct, t)
                yield
                t = S.op("pool", lambda e, L=L, ct=ct: e.tensor_tensor(out=catS[cur][:, 2 + ct, 0:n], in0=L["H"][:, 0:n], in1=L["g2"][:, 0:n], op=ALU.mult),
                         deps=[t, t_H, cat_wdeps])
                yield
                cat_parts.append(t)
                yield
        def norm_gen(ql):
            a0, a1 = PS.t[5], PS.t[6]
            tr0 = S.op("dve", lambda e: e.reciprocal(out=st[:, 3:4], in_=a0[:, 256:257]), deps=[PS.dep(5)])
            tr1 = S.op("dve", lambda e: e.reciprocal(out=st[:, 4:5], in_=a1[:, 256:257]), deps=[PS.dep(6)])
            tr1 = S.op("dve", lambda e: e.tensor_tensor(out=st[:, 4:5], in0=st[:, 4:5], in1=lamt[:, 4:5], op=ALU.mult), deps=[tr1, t_nlam])
            to = S.op("act", lambda e: e.activation(out=Ob[:, :], in_=a0[:, 0:256], func=AF.Copy, scale=st[:, 3:4]), deps=[tr0])
            PS.set(5, to)
            tat = S.op("dve", lambda e: e.scalar_tensor_tensor(out=ATb[:, :], in0=a1[:, 0:256], scalar=st[:, 4:5], in1=Ob[:, :],
                                                               op0=ALU.mult, op1=ALU.add), deps=[to, tr1])
            PS.set(6, tat)
            yield
            yield
            tsq = S.op("act", lambda e: e.activation(out=Yb[:, :], in_=ATb[:, :], func=AF.Square, accum_out=st[:, 5:6]),
                       deps=[tat, Y_u.take(0)])
            yield
            t1 = S.op("act", lambda e: e.activation(out=st[:, 6:7], in_=st[:, 5:6], func=AF.Ln, bias=epsb[:, 0:1], scale=1.0 / 256), deps=[tsq])
            yield
            t2 = S.op("act", lambda e: e.activation(out=st[:, 7:8], in_=st[:, 6:7], func=AF.Exp, scale=-0.5), deps=[t1])
            yield
            ty = S.op("dve", lambda e: e.scalar_tensor_tensor(out=Yb[:, :], in0=ATb[:, :], scalar=st[:, 7:8], in1=gsub[:, :],
                                                              op0=ALU.mult, op1=ALU.mult), deps=[t2, t_gs])
            yield
            bk = misc_bank()
            pv = PS.t[bk][:, 0:128].bitcast(BF16).rearrange("p (a t) -> p a t", a=2)
            tp = None
            for h in range(2):
                tp = S.op("pe", lambda e, h=h, pv=pv: e.transpose(out=pv[:, h, :], in_=Yb[:, h * 128:(h + 1) * 128], identity=identb[:, :]),
                          deps=[ty, PS.dep(bk) if h == 0 else None], signal=(h == 1))
            Y_u.add(0, tp)
            tc_ = S.op("act", lambda e, pv=pv, ql=ql: e.activation(out=catS[cur][:, 0:2, ql * 128:(ql + 1) * 128], in_=pv, func=AF.Copy),
                       deps=[tp, cat_wdeps])
            PS.set(bk, tc_)
            yield
            yield
            cat_parts.append(tc_)
            yield
        pending.extend([lru_gen(0), lru_gen(1)])
        if is_x:
            for ql in range(NTL):
                gq = ci * NTL + ql
                entries = [("meta", NMETA, 0, 0)] + [(kb, 128, NMETA + kb * 128, 1 + kb) for kb in range(gq + 1)]
                groups = [[entries[0]]] + [entries[i:i + 2] for i in range(1, len(entries), 2)]
                ng = len(groups)
                s_tok = [None] * ng
                e_tok = [None] * ng

                def emit_S(gi):
                    grp = groups[gi]
                    sbk = (3, 4)[gi % 2]
                    npg = grp[0][1]
                    bv = PS.t[sbk][:, :].rearrange("p (c l q) -> p c l q", c=2, l=2)
                    tk = None
                    first = True
                    for comp in range(2):
                        for li, (kb, np_, kpos, vblk) in enumerate(grp):
                            tk = S.op("pe", lambda e, bv=bv, comp=comp, li=li, np_=np_, kpos=kpos, ql=ql: e.matmul(
                                bv[0:np_, comp, li, :], lhsT=KTs[:, comp, kpos:kpos + np_], rhs=QTs[:, comp, ql * 128:(ql + 1) * 128],
                                start=True, stop=True),
                                      deps=[PS.dep(sbk), t_q, kready[kb], E_u.take(gi % 2)] if first else [kready[kb]],
                                      signal=(comp == 1 and li == len(grp) - 1))
                            first = False
                    PS.set(sbk, tk)
                    nl = len(grp)
                    ev = Eb[gi % 2]
                    te = S.op("act", lambda e, bv=bv, ev=ev, npg=npg, nl=nl: e.activation(out=ev[0:npg, :, 0:nl, :], in_=bv[0:npg, :, 0:nl, :],
                                                                                     func=AF.Exp, scale=ATT_SCALE),
                              deps=[tk])
                    PS.set(sbk, te)
                    if grp[-1][0] == gq:
                        li = nl - 1
                        te = S.op("pool", lambda e, ev=ev, li=li: e.tensor_tensor(out=ev[:, :, li, :], in0=ev[:, :, li, :],
                                                                                  in1=maskb[:, :].unsqueeze(1).to_broadcast([128, 2, 128]), op=ALU.mult),
                                  deps=[te, t_mk])
                    e_tok[gi] = te

                def emit_PV(gi):
                    grp = groups[gi]
                    ev = Eb[gi % 2]
                    tk = None
                    for comp in range(2):
                        abk = 5 + comp
                        for li, (kb, np_, kpos, vblk) in enumerate(grp):
                            first = (gi == 0 and li == 0)
                            last = (gi == ng - 1 and li == len(grp) - 1)
                            tk = S.op("pe", lambda e, abk=abk, ev=ev, comp=comp, li=li, np_=np_, vblk=vblk, first=first, last=last: e.matmul(
                                PS.t[abk][:, 0:257], lhsT=ev[0:np_, comp, li, :], rhs=Vs[0:np_, vblk, :], start=first, stop=last),
                                      deps=[e_tok[gi], vready[vblk], PS.dep(abk) if first else None],
                                      signal=(last or (comp == 1 and li == len(grp) - 1)))
                            if last:
                                PS.set(abk, tk)
                    E_u.add(gi % 2, tk)

                emit_S(0)
                for gi in range(ng):
                    if gi + 1 < ng:
                        emit_S(gi + 1)
                    emit_PV(gi)
                    pump(1)
                if prev_norm[0] is not None:
                    for _ in prev_norm[0]:
                        pass
                    if prev_norm[0] in pending:
                        pending.remove(prev_norm[0])
                ngen = norm_gen(ql)
                next(ngen)
                pending.append(ngen)
                prev_norm[0] = ngen
        while pending:
            pump(1)
        if pend_store[0] is not None:
            pcur, pcol, pdeps = pend_store[0]
            for tw in cat_store(pcur, pcol, pdeps):
                catS_u.add(pcur, tw)
            pend_store[0] = None
        if is_x:
            pend_store[0] = (cur, b * S_LEN + ci * CH, list(cat_parts))
            del cat_parts[:]
        return uT_next

    vready = {}
    kready = {}
    cat_parts = []
    chunks = [("x", b_, ci_) for b_ in range(NBATCH) for ci_ in range(S_LEN // CH)]
    uTw = step1("meta", 0, 0)
    uTw = process_chunk("meta", 0, 0, uTw, chunks[0])
    t_sm = S.op("dve", lambda e: e.tensor_copy(out=state_m[:, :], in_=state[:, :]), deps=[state_t[0], state_t[1]])
    t_hm = [S.op("dve", lambda e, i=i: e.tensor_copy(out=hist_m[:, i, :], in_=XR[i][:, 0:3]), deps=[hist_t[i]]) for i in range(2)]
    for b in range(NBATCH):
        if b > 0:
            t_rs = S.op("dve", lambda e: e.tensor_copy(out=state[:, :], in_=state_m[:, :]), deps=[t_sm, state_t[0], state_t[1]])
            state_t[0] = [t_rs]
            state_t[1] = [t_rs]
            for i in range(2):
                t_rh = S.op("dve", lambda e, i=i: e.tensor_copy(out=XR[i][:, 0:3], in_=hist_m[:, i, :]), deps=[t_hm[i], hist_t[i], XR_u.take(i)])
                hist_t[i] = [t_rh]
        for ci in range(S_LEN // CH):
            idx = b * (S_LEN // CH) + ci
            uTw = process_chunk("x", b, ci, uTw, chunks[idx + 1] if idx + 1 < len(chunks) else None)
    pcur, pcol, pdeps = pend_store[0]
    cat_store(pcur, pcol, pdeps)


def build_a():
    nc = bass.Bass("TRN2", target_bir_lowering=False)
    io = {}
    def inp(name, shape, dt):
        io[name] = nc.dram_tensor(name, shape, dt, kind="ExternalInput").ap()
    inp("x_all", [NBATCH * S_LEN, D], F32)
    inp("meta", [NMETA, D], F32)
    inp("w_in_c", [D, 1280], F32)
    inp("g1T", [128, KT], F32)
    inp("cosT", [128, TPOS], F32)
    inp("sinT", [128, TPOS], F32)
    inp("lamv", [4, 128], F32)
    inp("subg", [1, 256], F32)
    inp("lru_p", [128, 2, 8], F32)
    inp("w_r_c", [2, 128, 128], F32)
    inp("w_i_c", [2, 128, 128], F32)
    inp("mask", [128, 128], BF16)
    inp("identb", [128, 128], BF16)
    io["cat_out"] = nc.dram_tensor("cat_out", [512, NBATCH * S_LEN], BF16, kind="ExternalOutput").ap()
    with ExitStack() as stack:
        S = Sched(nc, stack)
        PS = Banks(nc, stack)
        AR = Arena(nc, stack)
        emit_phase_a(nc, S, stack, PS, io, AR)
        S.finalize()
        with nc.Block() as block:
            S.run(block)
    return nc


def rope_tables():
    inv_freq = (1.0 / (np.float32(10000.0) ** (np.arange(0, 128, 2, dtype=np.float32) / np.float32(128)))).astype(np.float32)
    ang = np.arange(TPOS, dtype=np.float32)[:, None] * inv_freq[None, :]
    cos = np.cos(ang).astype(np.float32).T
    sin = np.sin(ang).astype(np.float32).T
    return (np.ascontiguousarray(np.concatenate([cos, cos], axis=0)), np.ascontiguousarray(np.concatenate([sin, -sin], axis=0)))


def phase_a_inputs(inputs, cores):
    identb, _ = host_consts()
    mask = np.triu(np.ones((128, 128), np.float32)).astype(ml_dtypes.bfloat16)
    cosT, sinT = rope_tables()
    x = np.ascontiguousarray(np.asarray(inputs["x"])).reshape(NBATCH * S_LEN, D)
    w_in = np.asarray(inputs["w_in"][0])
    g1T = np.ascontiguousarray(np.asarray(inputs["mix_pre_g"][0]).reshape(KT, 128).T)
    lamv = np.stack([np.asarray(inputs[k][0]) for k in ("lambda_q1", "lambda_k1", "lambda_q2", "lambda_k2")]).astype(np.float32)
    maps = []
    for c in cores:
        cols = np.concatenate([np.arange(c * 256, (c + 1) * 256) + off for off in (0, 2048, 4096, 6144, 8192)])
        ch = slice(c * 256, (c + 1) * 256)
        lp = np.stack([np.asarray(inputs["conv_w"][0])[0, ch], np.asarray(inputs["conv_w"][0])[1, ch], np.asarray(inputs["conv_w"][0])[2, ch],
                       np.asarray(inputs["conv_w"][0])[3, ch], np.asarray(inputs["conv_b"][0])[ch], np.asarray(inputs["b_r"][0])[ch],
                       np.asarray(inputs["b_i"][0])[ch], np.asarray(inputs["lru_lambda"][0])[ch]], axis=-1)
        lp = np.ascontiguousarray(lp.reshape(2, 128, 8).transpose(1, 0, 2)).astype(np.float32)
        maps.append({
            "x_all": x,
            "meta": np.asarray(inputs["meta_tokens"]),
            "w_in_c": np.ascontiguousarray(w_in[:, cols]),
            "g1T": g1T,
            "cosT": cosT,
            "sinT": sinT,
            "lamv": lamv,
            "subg": np.asarray(inputs["subln_g"]).reshape(1, 256),
            "lru_p": lp,
            "w_r_c": np.ascontiguousarray(np.asarray(inputs["w_r"][0])[2 * c:2 * c + 2]),
            "w_i_c": np.ascontiguousarray(np.asarray(inputs["w_i"][0])[2 * c:2 * c + 2]),
            "mask": mask,
            "identb": identb,
        })
    return maps


def run_phase_a(inputs, cores=None):
    cores = list(range(NCORES)) if cores is None else cores
    nc = build_a()
    maps = phase_a_inputs(inputs, cores)
    res = run_bass_kernel_spmd(nc, maps, core_ids=list(range(len(cores))))
    return [np.asarray(r["cat_out"]) for r in res.results]


def build_fused():
    nc = bass.Bass("TRN2", target_bir_lowering=False)
    io = {}

    def inp(name, shape, dt):
        io[name] = nc.dram_tensor(name, shape, dt, kind="ExternalInput").ap()

    inp("x_all", [NBATCH * S_LEN, D], F32)
    inp("meta", [NMETA, D], F32)
    inp("w_in_c", [D, 1280], F32)
    inp("g1T", [128, KT], F32)
    inp("cosT", [128, TPOS], F32)
    inp("sinT", [128, TPOS], F32)
    inp("lamv", [4, 128], F32)
    inp("subg", [1, 256], F32)
    inp("lru_p", [128, 2, 8], F32)
    inp("w_r_c", [2, 128, 128], F32)
    inp("w_i_c", [2, 128, 128], F32)
    inp("mask", [128, 128], BF16)
    inp("identb", [128, 128], BF16)
    inp("identf", [128, 128], F32)
    inp("xrows", [TOK_PER_CORE, D], F32)
    inp("w_out", [D, D], F32)
    inp("w_gate", [D, FF], F32)
    inp("w_up", [D, FF], F32)
    inp("w_down", [FF, D], F32)
    inp("g_post", [1, D], F32)
    inp("g_fpost", [1, D], F32)
    inp("g2T", [128, KT], F32)
    io["out"] = nc.dram_tensor("out", [TOK_PER_CORE, D], F32, kind="ExternalOutput").ap()
    io["mixed_s"] = nc.dram_tensor("mixed_s", [TOK_PER_CORE, D], F32).ap()
    io["f_s"] = nc.dram_tensor("f_s", [TOK_PER_CORE, D], F32).ap()
    io["h1_s"] = nc.dram_tensor("h1_s", [TOK_PER_CORE, D], F32).ap()
    snd_attn = nc.dram_tensor("snd_attn", [NCORES * 256, TOK_PER_CORE], BF16)
    snd_rec = nc.dram_tensor("snd_rec", [NCORES * 256, TOK_PER_CORE], BF16)
    ag_attn = nc.dram_tensor("ag_attn", [NCORES * NCORES * 256, TOK_PER_CORE], BF16)
    ag_rec = nc.dram_tensor("ag_rec", [NCORES * NCORES * 256, TOK_PER_CORE], BF16)
    io["snd_attn"], io["snd_rec"] = snd_attn.ap(), snd_rec.ap()
    io["ag_attn"], io["ag_rec"] = ag_attn.ap(), ag_rec.ap()
    io["catT"] = None
    io["cat_out"] = None
    io["wout_rt"] = None
    with ExitStack() as stack:
        S = Sched(nc, stack)
        PS = Banks(nc, stack)
        AR = Arena(nc, stack)
        emit_phase_a(nc, S, stack, PS, io, AR, fused=True)
        done_a = S.all_done_tokens()
        grp = [list(range(NCORES))]
        t_ag1 = S.collective(lambda e: e.collective_compute("AllGather", ALU.bypass, replica_groups=grp,
                                                            ins=[snd_attn.ap().opt()], outs=[ag_attn.ap().opt()]), deps=done_a)
        t_ag2 = S.collective(lambda e: e.collective_compute("AllGather", ALU.bypass, replica_groups=grp,
                                                            ins=[snd_rec.ap().opt()], outs=[ag_rec.ap().opt()]), deps=[t_ag1])
        S.barrier(done_a)
        AR.reset()
        emit_phase_b(nc, S, stack, PS, io, AR, cat_ready=[t_ag1, t_ag2], fused=True)
        S.finalize()
        with nc.Block() as block:
            S.run(block)
    return nc


_NC_CACHE = {}


def kernel(**inputs):
    if "nc" not in _NC_CACHE:
        _NC_CACHE["nc"] = build_fused()
    nc = _NC_CACHE["nc"]
    cores = list(range(NCORES))
    maps = phase_a_inputs(inputs, cores)
    _, identf = host_consts()
    x = np.ascontiguousarray(np.asarray(inputs["x"])).reshape(NBATCH * S_LEN, D)
    g2T = np.ascontiguousarray(np.asarray(inputs["ffn_pre_g"][0]).reshape(KT, 128).T)
    for c in cores:
        sl = slice(c * TOK_PER_CORE, (c + 1) * TOK_PER_CORE)
        maps[c].update({
            "identf": identf,
            "xrows": x[sl],
            "w_out": np.asarray(inputs["w_out"][0]),
            "w_gate": np.asarray(inputs["w_gate"][0]),
            "w_up": np.asarray(inputs["w_up"][0]),
            "w_down": np.asarray(inputs["w_down"][0]),
            "g_post": np.asarray(inputs["mix_post_g"]).reshape(1, D),
            "g_fpost": np.asarray(inputs["ffn_post_g"]).reshape(1, D),
            "g2T": g2T,
        })
    res = run_bass_kernel_spmd(nc, maps, core_ids=cores)
    outs = [np.asarray(r["out"]) for r in res.results]
    return np.concatenate(outs, axis=0).reshape(NBATCH, S_LEN, D).astype(np.float32)
```
